# Optimizing a Trainium2 kernel written in Bass

```python
import math
import jax, jax.numpy as jnp
from jax import lax
import numpy as np

D_MODEL = 1024
BATCH = 1
SEQ = 16384
DEPTH = 4
DEC_BATCH = 16
DEC_SEQ = 32
PAST_LEN = 4096

CHUNK = 64
N_MIXERS = 3
N_GDN = (DEPTH + 2) // 3
N_SB = (DEPTH + 1) // 3
N_RET = DEPTH // 3
GDN_HEADS = 8
GDN_DK = 128
GDN_DV = 128
GDN_CONV = 4
GDN_QKV = 2 * GDN_HEADS * GDN_DK + GDN_HEADS * GDN_DV
SB_HEADS = 16
SB_DH = 64
SB_BLOCK = 128
RET_HEADS = 4
RET_DK = 256
RET_DV = 512
RET_ROPE_BASE = 10000.0
D_FF = 2816
FFN_CONV = 3
NORM_EPS = 1e-6
F32 = jnp.float32

kernel_name = 'hybrid_gdn_stickbreak_retention_stream_step'


def rms_norm(x, g):
    xf = x.astype(F32)
    y = xf * lax.rsqrt(jnp.mean(xf * xf, axis=-1, keepdims=True) + NORM_EPS)
    return (y * g.astype(F32)).astype(x.dtype)


def l2_normalize(x):
    xf = x.astype(F32)
    return xf * lax.rsqrt(jnp.sum(xf * xf, axis=-1, keepdims=True) + NORM_EPS)


def head_group_norm(o, g):
    mu = jnp.mean(o, axis=-1, keepdims=True)
    d = o - mu
    var = jnp.mean(d * d, axis=-1, keepdims=True)
    return d * lax.rsqrt(var + NORM_EPS) * g.astype(F32).reshape(o.shape[2:])


def causal_depthwise_conv(x, buf, w):
    width = w.shape[0]
    T = x.shape[1]
    xp = jnp.concatenate([buf.astype(x.dtype), x], axis=1)
    y = xp[:, 0:T] * w[0]
    for j in range(1, width):
        y = y + xp[:, j:j + T] * w[j]
    return y, xp[:, xp.shape[1] - (width - 1):]


def to_chunks(x, chunk):
    B, T, H = x.shape[:3]
    x = x.astype(F32).reshape((B, T // chunk, chunk, H) + x.shape[3:])
    return jnp.moveaxis(x, (1, 3), (0, 2))


def from_chunks(o):
    N, B, H, C = o.shape[:4]
    return jnp.moveaxis(o, (0, 2), (1, 3)).reshape((B, N * C, H) + o.shape[4:])


def rotary(x, pos):
    half = x.shape[-1] // 2
    inv_freq = RET_ROPE_BASE ** (-jnp.arange(half, dtype=F32) / half)
    ang = pos.astype(F32)[:, None] * inv_freq[None, :]
    cos = jnp.cos(ang)[None, :, None, :]
    sin = jnp.sin(ang)[None, :, None, :]
    xf = x.astype(F32)
    x1, x2 = xf[..., :half], xf[..., half:]
    return jnp.concatenate([x1 * cos - x2 * sin, x1 * sin + x2 * cos], axis=-1)


def gated_delta_rule(q, k, v, g, beta, S0, chunk):
    qc, kc, vc = to_chunks(q, chunk), to_chunks(k, chunk), to_chunks(v, chunk)
    gc, bc = to_chunks(g, chunk), to_chunks(beta, chunk)
    G = jnp.cumsum(gc, axis=-1)
    idx = jnp.arange(chunk)
    incl = idx[:, None] >= idx[None, :]
    strict = idx[:, None] > idx[None, :]
    decay = jnp.exp(jnp.where(incl, G[..., :, None] - G[..., None, :], -jnp.inf))
    kb = kc * bc[..., None]
    L = jnp.where(strict, jnp.einsum('nbhik,nbhjk->nbhij', kb, kc) * decay, 0.0)
    eye = jnp.eye(chunk, dtype=F32)
    Tinv = lax.linalg.triangular_solve(eye + L, jnp.broadcast_to(eye, L.shape), left_side=True, lower=True)
    u = jnp.einsum('nbhij,nbhjv->nbhiv', Tinv, vc * bc[..., None])
    w = jnp.einsum('nbhij,nbhjk->nbhik', Tinv, kb * jnp.exp(G)[..., None])
    qk = jnp.einsum('nbhik,nbhjk->nbhij', qc, kc) * decay
    q_dec = qc * jnp.exp(G)[..., None]
    k_dec = kc * jnp.exp(G[..., -1:] - G)[..., None]
    g_last = jnp.exp(G[..., -1])

    def step(S, inp):
        u_n, w_n, qk_n, qd_n, kd_n, gl_n = inp
        v_new = u_n - jnp.einsum('bhck,bhkv->bhcv', w_n, S)
        o = jnp.einsum('bhck,bhkv->bhcv', qd_n, S) + jnp.einsum('bhij,bhjv->bhiv', qk_n, v_new)
        S = S * gl_n[..., None, None] + jnp.einsum('bhck,bhcv->bhkv', kd_n, v_new)
        return S, o

    S, o = lax.scan(step, S0.astype(F32), (u, w, qk, q_dec, k_dec, g_last))
    return from_chunks(o), S


def gdn_mixer(h, S0, conv_buf, w_in, conv_w, A_log, dt_bias, norm_g, w_out):
    B, T, _ = h.shape
    chunk = min(CHUNK, T)
    hv = GDN_HEADS * GDN_DV
    qkv, z, a, b = jnp.split(h @ w_in, [GDN_QKV, GDN_QKV + hv, GDN_QKV + hv + GDN_HEADS], axis=-1)
    qkv, new_buf = causal_depthwise_conv(qkv, conv_buf, conv_w)
    qkv = jax.nn.silu(qkv)
    q, k, v = jnp.split(qkv, [GDN_HEADS * GDN_DK, 2 * GDN_HEADS * GDN_DK], axis=-1)
    q = l2_normalize(q.reshape(B, T, GDN_HEADS, GDN_DK)) * GDN_DK ** -0.5
    k = l2_normalize(k.reshape(B, T, GDN_HEADS, GDN_DK))
    v = v.reshape(B, T, GDN_HEADS, GDN_DV)
    beta = jax.nn.sigmoid(b.astype(F32))
    g = -jnp.exp(A_log.astype(F32)) * jax.nn.softplus(a.astype(F32) + dt_bias.astype(F32))
    o, S = gated_delta_rule(q, k, v, g, beta, S0, chunk)
    o = rms_norm(o, norm_g) * jax.nn.silu(z.astype(F32)).reshape(B, T, GDN_HEADS, GDN_DV)
    return o.reshape(B, T, hv).astype(h.dtype) @ w_out, S.astype(S0.dtype), new_buf


def stick_breaking_attend(q, k, v, q_start):
    B, Tq, H, Dh = q.shape
    Tk = k.shape[1]
    blk = min(SB_BLOCK, Tq)
    nb = Tq // blk
    kf = k.astype(F32)
    vf = v.astype(F32)
    key_pos = jnp.arange(Tk)
    scale = Dh ** -0.5

    def one_block(args):
        qb, start = args
        z = jnp.einsum('bqhd,bkhd->bhqk', qb.astype(F32), kf) * scale
        q_pos = start + jnp.arange(blk)
        mask = key_pos[None, :] < q_pos[:, None]
        log_not = jnp.where(mask, jax.nn.log_sigmoid(-z), 0.0)
        after = lax.cumsum(log_not, axis=3, reverse=True) - log_not
        wts = jnp.where(mask, jnp.exp(jax.nn.log_sigmoid(z) + after), 0.0)
        return jnp.einsum('bhqk,bkhd->bqhd', wts, vf)

    qb = jnp.moveaxis(q.reshape(B, nb, blk, H, Dh), 1, 0)
    starts = q_start + blk * jnp.arange(nb)
    o = lax.map(one_block, (qb, starts))
    return jnp.moveaxis(o, 0, 1).reshape(B, Tq, H, Dh)


def sb_mixer(h, past_k, past_v, w_in, q_norm_g, k_norm_g, w_out):
    B, T, _ = h.shape
    q, k, v = jnp.split(h @ w_in, 3, axis=-1)
    q = rms_norm(q.reshape(B, T, SB_HEADS, SB_DH), q_norm_g)
    k = rms_norm(k.reshape(B, T, SB_HEADS, SB_DH), k_norm_g)
    v = v.reshape(B, T, SB_HEADS, SB_DH)
    k_all = jnp.concatenate([past_k.astype(k.dtype), k], axis=1)
    v_all = jnp.concatenate([past_v.astype(v.dtype), v], axis=1)
    o = stick_breaking_attend(q, k_all, v_all, past_k.shape[1])
    return o.reshape(B, T, SB_HEADS * SB_DH).astype(h.dtype) @ w_out, k, v


def retention_chunked(q, k, v, log_gamma, R0, chunk):
    qc, kc, vc = to_chunks(q, chunk), to_chunks(k, chunk), to_chunks(v, chunk)
    idx = jnp.arange(chunk, dtype=F32)
    rel = idx[:, None] - idx[None, :]
    lg = log_gamma[:, None, None]
    dmask = jnp.where(rel >= 0, jnp.exp(lg * jnp.maximum(rel, 0.0)), 0.0)
    inner = jnp.einsum('nbhij,nbhjv->nbhiv', jnp.einsum('nbhik,nbhjk->nbhij', qc, kc) * dmask, vc)
    q_dec = qc * jnp.exp(log_gamma[:, None] * (idx + 1.0))[:, :, None]
    k_dec = kc * jnp.exp(log_gamma[:, None] * (chunk - 1.0 - idx))[:, :, None]
    g_chunk = jnp.exp(log_gamma * chunk)[:, None, None]

    def step(R, inp):
        qd, kd, vn = inp
        o = jnp.einsum('bhck,bhkv->bhcv', qd, R)
        R = R * g_chunk + jnp.einsum('bhck,bhcv->bhkv', kd, vn)
        return R, o

    R, cross = lax.scan(step, R0.astype(F32), (q_dec, k_dec, vc))
    return from_chunks(inner + cross), R


def ret_mixer(h, R0, w_in, norm_g, w_out, pos0):
    B, T, _ = h.shape
    chunk = min(CHUNK, T)
    hk = RET_HEADS * RET_DK
    hv = RET_HEADS * RET_DV
    q, k, v, gt = jnp.split(h @ w_in, [hk, 2 * hk, 2 * hk + hv], axis=-1)
    pos = pos0 + jnp.arange(T)
    q = rotary(q.reshape(B, T, RET_HEADS, RET_DK), pos)
    k = rotary(k.reshape(B, T, RET_HEADS, RET_DK), pos) * RET_DK ** -0.5
    v = v.reshape(B, T, RET_HEADS, RET_DV)
    log_gamma = jnp.log1p(-jnp.exp2(-5.0 - jnp.arange(RET_HEADS, dtype=F32)))
    o, R = retention_chunked(q, k, v, log_gamma, R0, chunk)
    o = head_group_norm(o, norm_g) * jax.nn.silu(gt.astype(F32)).reshape(B, T, RET_HEADS, RET_DV)
    return o.reshape(B, T, hv).astype(h.dtype) @ w_out, R.astype(R0.dtype)


def conv_ffn(h, buf, w_in, conv_w, conv_b, w_out):
    u, new_buf = causal_depthwise_conv(h @ w_in, buf, conv_w)
    gate, val = jnp.split(u + conv_b, 2, axis=-1)
    return (jax.nn.silu(gate) * val) @ w_out, new_buf


def setup_inputs(seed: int = 0) -> dict:
    key = jax.random.key(seed)
    ks = iter(jax.random.split(key, 48))
    D = D_MODEL
    gdn_proj = GDN_QKV + GDN_HEADS * GDN_DV + 2 * GDN_HEADS
    ret_proj = 2 * RET_HEADS * RET_DK + 2 * RET_HEADS * RET_DV

    def nrm(shape, scale):
        return jax.random.normal(next(ks), shape, F32) * scale

    def gain(shape):
        return 1.0 + 0.02 * jax.random.normal(next(ks), shape, F32)

    dt = jnp.exp(jax.random.uniform(next(ks), (N_GDN, GDN_HEADS), F32, math.log(1e-3), math.log(1e-1)))
    dt_bias = dt + jnp.log(-jnp.expm1(-dt))
    A_log = jnp.log(jax.random.uniform(next(ks), (N_GDN, GDN_HEADS), F32, 1.0, 16.0))
    return {
        'x_prompt': nrm((BATCH, SEQ, D), 1.0),
        'x_sample': nrm((DEC_BATCH, DEC_SEQ, D), 1.0),
        'state_l0_gdn_S': nrm((DEC_BATCH, GDN_HEADS, GDN_DK, GDN_DV), 0.1),
        'state_l0_gdn_conv': nrm((DEC_BATCH, GDN_CONV - 1, GDN_QKV), 1.0),
        'cache_l1_sb_k': nrm((DEC_BATCH, PAST_LEN, SB_HEADS, SB_DH), 1.0),
        'cache_l1_sb_v': nrm((DEC_BATCH, PAST_LEN, SB_HEADS, SB_DH), 1.0),
        'state_l2_ret': nrm((DEC_BATCH, RET_HEADS, RET_DK, RET_DV), 0.1),
        'state_l3_gdn_S': nrm((DEC_BATCH, GDN_HEADS, GDN_DK, GDN_DV), 0.1),
        'state_l3_gdn_conv': nrm((DEC_BATCH, GDN_CONV - 1, GDN_QKV), 1.0),
        'state_ffn_conv': nrm((DEPTH, DEC_BATCH, FFN_CONV - 1, 2 * D_FF), 1.0),
        'c_prompt': nrm((BATCH, D), 1.0),
        'c_sample': nrm((DEC_BATCH, D), 1.0),
        'ada_w': nrm((DEPTH, D, 6 * D), D ** -0.5),
        'ada_b': nrm((DEPTH, 6 * D), 0.02),
        'norm_mix_g': gain((DEPTH, D)),
        'norm_ffn_g': gain((DEPTH, D)),
        'gdn_w_in': nrm((N_GDN, D, gdn_proj), D ** -0.5),
        'gdn_conv_w': nrm((N_GDN, GDN_CONV, GDN_QKV), GDN_CONV ** -0.5),
        'gdn_A_log': A_log,
        'gdn_dt_bias': dt_bias,
        'gdn_norm_g': gain((N_GDN, GDN_DV)),
        'gdn_w_out': nrm((N_GDN, GDN_HEADS * GDN_DV, D), (GDN_HEADS * GDN_DV) ** -0.5),
        'sb_w_in': nrm((N_SB, D, 3 * SB_HEADS * SB_DH), D ** -0.5),
        'sb_q_norm_g': gain((N_SB, SB_DH)),
        'sb_k_norm_g': gain((N_SB, SB_DH)),
        'sb_w_out': nrm((N_SB, SB_HEADS * SB_DH, D), (SB_HEADS * SB_DH) ** -0.5),
        'ret_w_in': nrm((N_RET, D, ret_proj), D ** -0.5),
        'ret_norm_g': gain((N_RET, RET_HEADS * RET_DV)),
        'ret_w_out': nrm((N_RET, RET_HEADS * RET_DV, D), (RET_HEADS * RET_DV) ** -0.5),
        'ffn_w_in': nrm((DEPTH, D, 2 * D_FF), D ** -0.5),
        'ffn_conv_w': nrm((DEPTH, FFN_CONV, 2 * D_FF), FFN_CONV ** -0.5),
        'ffn_conv_b': nrm((DEPTH, 2 * D_FF), 0.02),
        'ffn_w_out': nrm((DEPTH, D_FF, D), D_FF ** -0.5),
    }


def reference(x_prompt, x_sample, state_l0_gdn_S, state_l0_gdn_conv, cache_l1_sb_k, cache_l1_sb_v,
              state_l2_ret, state_l3_gdn_S, state_l3_gdn_conv, state_ffn_conv, c_prompt, c_sample,
              ada_w, ada_b, norm_mix_g, norm_ffn_g,
              gdn_w_in, gdn_conv_w, gdn_A_log, gdn_dt_bias, gdn_norm_g, gdn_w_out,
              sb_w_in, sb_q_norm_g, sb_k_norm_g, sb_w_out,
              ret_w_in, ret_norm_g, ret_w_out,
              ffn_w_in, ffn_conv_w, ffn_conv_b, ffn_w_out):

    def run_group(x, c, mixer_states, ffn_bufs, pos0):
        new_mixer_states = []
        new_ffn = []
        for i in range(DEPTH):
            kind, j = i % N_MIXERS, i // N_MIXERS
            mod = jax.nn.silu(c) @ ada_w[i] + ada_b[i]
            sh1, sc1, g1, sh2, sc2, g2 = [m[:, None, :] for m in jnp.split(mod, 6, axis=-1)]
            h = rms_norm(x, norm_mix_g[i]) * (1 + sc1) + sh1
            if kind == 0:
                S0, buf = mixer_states[i]
                out, S, buf = gdn_mixer(h, S0, buf, gdn_w_in[j], gdn_conv_w[j], gdn_A_log[j],
                                        gdn_dt_bias[j], gdn_norm_g[j], gdn_w_out[j])
                new = (S, buf)
            elif kind == 1:
                pk, pv = mixer_states[i]
                out, k, v = sb_mixer(h, pk, pv, sb_w_in[j], sb_q_norm_g[j], sb_k_norm_g[j], sb_w_out[j])
                new = (k, v)
            else:
                (R0,) = mixer_states[i]
                out, R = ret_mixer(h, R0, ret_w_in[j], ret_norm_g[j], ret_w_out[j], pos0)
                new = (R,)
            x = x + g1 * out
            h = rms_norm(x, norm_ffn_g[i]) * (1 + sc2) + sh2
            out, fbuf = conv_ffn(h, ffn_bufs[i], ffn_w_in[i], ffn_conv_w[i], ffn_conv_b[i], ffn_w_out[i])
            x = x + g2 * out
            new_mixer_states.append(new)
            new_ffn.append(fbuf)
        return x, new_mixer_states, jnp.stack(new_ffn)

    B, dt = x_prompt.shape[0], x_prompt.dtype
    zero_states = []
    for i in range(DEPTH):
        kind = i % N_MIXERS
        if kind == 0:
            zero_states.append((jnp.zeros((B, GDN_HEADS, GDN_DK, GDN_DV), dt),
                                jnp.zeros((B, GDN_CONV - 1, GDN_QKV), dt)))
        elif kind == 1:
            zero_states.append((jnp.zeros((B, 0, SB_HEADS, SB_DH), dt),
                                jnp.zeros((B, 0, SB_HEADS, SB_DH), dt)))
        else:
            zero_states.append((jnp.zeros((B, RET_HEADS, RET_DK, RET_DV), dt),))
    zero_ffn = jnp.zeros((DEPTH, B, FFN_CONV - 1, 2 * D_FF), dt)
    y_prompt, p_states, p_ffn_conv = run_group(x_prompt, c_prompt, zero_states, zero_ffn, 0)

    sample_states = [(state_l0_gdn_S, state_l0_gdn_conv), (cache_l1_sb_k, cache_l1_sb_v),
                     (state_l2_ret,), (state_l3_gdn_S, state_l3_gdn_conv)]
    past_len = cache_l1_sb_k.shape[1]
    y_sample, s_states, s_ffn_conv = run_group(x_sample, c_sample, sample_states, state_ffn_conv, past_len)

    (p_l0_S, p_l0_conv), (p_l1_k, p_l1_v), (p_l2_R,), (p_l3_S, p_l3_conv) = p_states
    (s_l0_S, s_l0_conv), (s_l1_k, s_l1_v), (s_l2_R,), (s_l3_S, s_l3_conv) = s_states
    return (y_prompt, y_sample,
            p_l0_S, p_l0_conv, p_l1_k, p_l1_v, p_l2_R, p_l3_S, p_l3_conv, p_ffn_conv,
            s_l0_S, s_l0_conv, s_l1_k, s_l1_v, s_l2_R, s_l3_S, s_l3_conv, s_ffn_conv)
```

```python
import numpy as np
from contextlib import ExitStack
import concourse.bass as bass
import concourse.mybir as mybir
from concourse.bass_utils import run_bass_kernel_spmd

F32 = mybir.dt.float32
BF16 = mybir.dt.bfloat16
ALU = mybir.AluOpType
AF = mybir.ActivationFunctionType

NCORE = 8
D = 1024
SEQ = 16384
NSMP = 16
SL = 32
PAST = 4096
DFF = 2816
FOWN = 352
EPS = 1e-6
EPOCH = 30000
NSLOT = 24
NTP = 256
DBG = {}


class T:
    __slots__ = ("t", "lw", "rd", "name")

    def __init__(self, t, name=""):
        self.t = t
        self.lw = None
        self.rd = {}
        self.name = name

    def __getitem__(self, k):
        return self.t[k]


class Prog:
    ENG = ["pe", "dve", "act", "pool", "sp"]

    def __init__(self, nc, stack):
        self.nc = nc
        self.stack = stack
        self.ops = {e: [] for e in self.ENG}
        self.cnt = {e: 0 for e in self.ENG}
        self.sems = {}
        self.water = {e: {} for e in self.ENG}
        self.slot_i = {"sp": 0, "pool": 0, "act": 0}
        self.slot_val = {}
        self.nsem = 0
        self.ncc = 0
        self.nbuf = 0

    def sb(self, shape, dt, name=None):
        self.nbuf += 1
        nby = int(np.prod(shape[1:])) * (4 if dt in (F32, mybir.dt.int32) else 2)
        self.sbytes = getattr(self, "sbytes", 0) + ((nby + 31) // 32) * 32
        self.big = getattr(self, "big", [])
        self.big.append((nby, name or "b%d" % self.nbuf, tuple(shape)))
        name = name or "b%d" % self.nbuf
        return T(self.stack.enter_context(self.nc.sbuf_tensor(name, list(shape), dt)), name)

    def ps(self, shape, dt=F32, name=None):
        self.nbuf += 1
        name = name or "p%d" % self.nbuf
        return T(self.stack.enter_context(self.nc.psum_tensor(name, list(shape), dt)), name)

    def sem(self, key):
        if key not in self.sems:
            self.nsem += 1
            self.sems[key] = self.stack.enter_context(self.nc.semaphore("s%d" % self.nsem))
        return self.sems[key]

    def _need(self, e, reads, writes):
        need = {}

        def add(p):
            if p is not None and need.get(p[0], 0) < p[1]:
                need[p[0]] = p[1]
        for t in reads:
            add(t.lw)
        for t in writes:
            add(t.lw)
            for k, v in t.rd.items():
                add((k, v))
        out = []
        for k, v in need.items():
            if k[0] == e and e == "pe":
                continue
            if self.water[e].get(k, 0) >= v:
                continue
            self.water[e][k] = v
            out.append((self.sem(k), v))
        return out

    def op(self, e, fn, reads=(), writes=()):
        waits = self._need(e, reads, writes)
        self.cnt[e] += 1
        n = self.cnt[e]
        key = (e, (n - 1) // EPOCH)
        val = (n - 1) % EPOCH + 1
        sem = self.sem(key)

        def run(eng, waits=waits, fn=fn, sem=sem):
            for s, v in waits:
                eng.wait_ge(s, v)
            fn(eng).then_inc(sem, 1)
        self.ops[e].append(run)
        for t in writes:
            t.lw = (key, val)
            t.rd = {}
        for t in reads:
            if t not in writes:
                t.rd[key] = val

    def dma(self, q, out, in_, reads=(), writes=()):
        i = self.slot_i[q]
        self.slot_i[q] += 1
        key = ("dma", q, i % NSLOT)
        prev = self.slot_val.get(key, 0)
        waits = self._need(q, reads, writes)
        sem = self.sem(key)
        if prev > 0 and self.water[q].get(key, 0) < prev:
            self.water[q][key] = prev
            waits.append((sem, prev))
        val = prev + 16
        self.slot_val[key] = val

        def run(eng, waits=waits, sem=sem, out=out, in_=in_):
            for s, v in waits:
                eng.wait_ge(s, v)
            eng.dma_start(out=out, in_=in_).then_inc(sem, 16)
        self.ops[q].append(run)
        for t in writes:
            t.lw = (key, val)
            t.rd = {}
        for t in reads:
            t.rd[key] = val

    def cc(self, ins_t, outs_t, ins_ap, outs_ap):
        e = "pool"
        waits = self._need(e, ins_t, outs_t)
        self.ncc += 1
        key = ("cc", 0)
        val = self.ncc
        sem = self.sem(key)

        if DBG.get("nocc"):
            a, b = DBG["cc_bufs"][id(ins_t[0])]
            rows = a.shape[0]
            for r8 in range(NCORE):
                self.dma("sp", b.ap()[r8 * rows:(r8 + 1) * rows, :], a.ap(), reads=ins_t, writes=outs_t)
            self.ncc -= 1
            return

        def run(eng, waits=waits, sem=sem):
            for s, v in waits:
                eng.wait_ge(s, v)
            eng.collective_compute("AllGather", ALU.bypass, replica_groups=[list(range(NCORE))],
                                   ins=ins_ap, outs=outs_ap).then_inc(sem)
        self.ops[e].append(run)
        for t in outs_t:
            t.lw = (key, val)
            t.rd = {}
        for t in ins_t:
            t.rd[key] = val

    def wait_all(self, e, tiles):
        waits = self._need(e, [], tiles)

        def run(eng, waits=waits):
            for s, v in waits:
                eng.wait_ge(s, v)
        self.ops[e].append(run)

    def drain(self):
        finals = {}
        for e in self.ENG:
            n = self.cnt[e]
            if n > 0:
                finals[(e, (n - 1) // EPOCH)] = (n - 1) % EPOCH + 1
        for k, v in self.slot_val.items():
            finals[k] = v
        if self.ncc:
            finals[("cc", 0)] = self.ncc
        for q in ("sp", "pool", "act", "dve", "pe"):
            waits = [(self.sem(k), v) for k, v in finals.items() if k[0] != q or k[0] == "dma"]

            def run(eng, waits=waits):
                for s_, v in waits:
                    eng.wait_ge(s_, v)
            self.ops[q].append(run)

    def emit(self):
        with self.nc.Block() as block:
            @block.tensor
            def _(eng):
                for f in self.ops["pe"]:
                    f(eng)

            @block.vector
            def _(eng):
                for f in self.ops["dve"]:
                    f(eng)

            @block.scalar
            def _(eng):
                for f in self.ops["act"]:
                    f(eng)

            @block.gpsimd
            def _(eng):
                for f in self.ops["pool"]:
                    f(eng)

            @block.sync
            def _(eng):
                for f in self.ops["sp"]:
                    f(eng)


def _km(w):
    K, Fd = w.shape
    return np.ascontiguousarray(w.reshape(K // 128, 128, Fd).transpose(1, 0, 2))


def _fm(v):
    return np.ascontiguousarray(v.reshape(-1, 128).T)


def host_consts():
    c = {}
    c["ident"] = np.eye(128, dtype=np.float32)
    b = np.zeros((128, 128), np.float32)
    b[:64, :64] = 1
    b[64:, 64:] = 1
    c["blk64"] = b
    j = np.arange(128)[:, None]
    i = np.arange(128)[None, :]
    c["triI"] = (j <= i).astype(np.float32)
    c["triS"] = (j < i).astype(np.float32)
    c["negU"] = -(j >= i).astype(np.float32)
    q = np.arange(NTP)[None, None, :]
    kb = np.arange(NTP // 128)[None, :, None]
    s = np.arange(128)[:, None, None]
    c["maskP"] = (kb * 128 + s < q).astype(np.float32)
    r64 = np.ones((1, 512), np.float32)
    r64[0, ::64] = 0
    c["reset64"] = r64
    r32 = np.ones((1, 32), np.float32)
    r32[0, 0] = 0
    c["reset32"] = r32
    for Cn in (64, 32):
        nlev = int(np.log2(Cn))
        jj = np.arange(Cn)[:, None]
        ii = np.arange(Cn)[None, :]
        mu = np.zeros((Cn, nlev, Cn), np.float32)
        for k in range(nlev):
            s_ = 2 ** k
            mu[:, k, :] = ((jj // (2 * s_) == ii // (2 * s_)) & (jj % (2 * s_) < s_) & (ii % (2 * s_) >= s_))
        reps = (NTP // Cn) if Cn == 64 else 1
        U = np.tile(mu, (1, 1, reps))
        Lm = np.tile(mu.transpose(2, 1, 0), (1, 1, reps))
        pad = np.zeros((128, nlev, reps * Cn), np.float32)
        pad[:Cn] = U
        c["lvU%d" % Cn] = pad.copy()
        pad[:Cn] = Lm
        c["lvL%d" % Cn] = pad.copy()
        idr = np.zeros((128, reps * Cn), np.float32)
        idr[:Cn] = np.tile(np.eye(Cn, dtype=np.float32), (1, reps))
        c["idrep%d" % Cn] = idr
    c["invf"] = (10000.0 ** (-np.arange(128, dtype=np.float32) / 128)).astype(np.float32)[:, None]
    c["iota"] = np.broadcast_to(np.arange(512, dtype=np.float32)[None, :], (128, 512)).copy()
    return c


def host_core_inputs(inp, r):
    o = {}
    f = np.float32
    sel = np.concatenate([np.arange(0, 2048), np.arange(3072, 5120)])
    o["adaw"] = np.stack([_km(inp["ada_w"][l][:, sel]) for l in range(4)])
    o["adab"] = np.stack([_fm(inp["ada_b"][l][sel]) for l in range(4)], axis=1)
    own = []
    ownb = []
    for l in range(4):
        for base in (2048, 5120):
            own.append(inp["ada_w"][l][:, base + 128 * r: base + 128 * r + 128])
            ownb.append(inp["ada_b"][l][base + 128 * r: base + 128 * r + 128])
    o["adaw_own"] = _km(np.concatenate(own, axis=1))
    o["adab_own"] = np.stack(ownb, axis=1)
    ng = np.zeros((128, 4, 2, 8), f)
    for l in range(4):
        ng[:, l, 0] = _fm(inp["norm_mix_g"][l])
        ng[:, l, 1] = _fm(inp["norm_ffn_g"][l])
    o["normg"] = ng
    call = np.concatenate([inp["c_prompt"], inp["c_sample"]], axis=0)
    o["cT"] = np.ascontiguousarray(call.T.reshape(8, 128, 17).transpose(1, 0, 2))
    h = r
    gw, gab, gcv, gsc, gng, gwo = [], [], [], [], [], []
    for j in range(2):
        W = inp["gdn_w_in"][j]
        cols = np.concatenate([np.arange(128 * h, 128 * h + 128), 1024 + np.arange(128 * h, 128 * h + 128),
                               2048 + np.arange(128 * h, 128 * h + 128), 3072 + np.arange(128 * h, 128 * h + 128)])
        gw.append(_km(W[:, cols]))
        gab.append(_km(W[:, [4096 + h, 4104 + h]]))
        cw = inp["gdn_conv_w"][j]
        gcv.append(np.stack([cw[:, 128 * h + 1024 * w: 128 * h + 1024 * w + 128].T for w in range(3)], axis=1))
        gsc.append(np.broadcast_to(np.array([inp["gdn_A_log"][j][h], inp["gdn_dt_bias"][j][h]], f)[None, :], (128, 2)))
        gng.append(inp["gdn_norm_g"][j])
        gwo.append(_km(inp["gdn_w_out"][j][:, 128 * r:128 * r + 128]))
    o["gw_in"] = np.stack(gw)
    o["gw_ab"] = np.stack(gab)
    o["gconv"] = np.stack(gcv, axis=1)
    o["gsc"] = np.stack(gsc, axis=1).astype(f)
    o["gng"] = np.stack(gng, axis=1)
    o["gw_out"] = np.stack(gwo)
    W = inp["sb_w_in"][0]
    cols = np.concatenate([np.arange(128 * r, 128 * r + 128) + 1024 * w for w in range(3)])
    o["sw_in"] = _km(W[:, cols])
    o["sng"] = np.stack([np.tile(inp["sb_q_norm_g"][0], 2), np.tile(inp["sb_k_norm_g"][0], 2)], axis=1)
    o["sw_out"] = _km(inp["sb_w_out"][0][:, 128 * r:128 * r + 128])
    hh, half = r // 2, r % 2
    perm = np.concatenate([np.arange(256 * half, 256 * half + 256), np.arange(256 * (1 - half), 256 * (1 - half) + 256)])
    W = inp["ret_w_in"][0]
    cols = np.concatenate([256 * hh + np.arange(256), 1024 + 256 * hh + np.arange(256),
                           2048 + 512 * hh + perm, 4096 + 512 * hh + perm])
    o["rw_in"] = _km(W[:, cols])
    o["rng"] = _fm(inp["ret_norm_g"][0][512 * hh + perm])
    o["rw_out"] = _km(inp["ret_w_out"][0][:, 128 * r:128 * r + 128])
    lg = np.log1p(-np.exp2(-5.0 - hh))
    rc = np.zeros((128, 512 + 32 + 64 + 32 + 2 + 2), f)
    idx64 = np.arange(64)
    idx32 = np.arange(32)
    rc[:, 0:512] = np.tile(np.exp(lg * (idx64 + 1.0)), 8)[None, :]
    rc[:, 512:544] = np.exp(lg * (idx32 + 1.0))[None, :]
    jj = np.arange(64)[:, None]
    ii = np.arange(64)[None, :]
    rc[:64, 544:608] = np.where(jj <= ii, np.exp(lg * np.maximum(ii - jj, 0)), 0.0)
    rc[:32, 608:640] = rc[:32, 544:576]
    rc[:64, 640] = np.exp(lg * (63.0 - idx64))
    rc[:32, 641] = np.exp(lg * (31.0 - idx32))
    rc[:, 642] = np.exp(lg * 64.0)
    rc[:, 643] = np.exp(lg * 32.0)
    o["rconst"] = rc
    fw, fcv, fcb, fwo = [], [], [], []
    rows = np.zeros((24 * 128,), np.int64)
    rvalid = np.zeros((24 * 128,), bool)
    for rr in range(8):
        for c3 in range(3):
            for p in range(128):
                k = 128 * c3 + p
                if k < FOWN:
                    rows[(rr * 3 + c3) * 128 + p] = FOWN * rr + k
                    rvalid[(rr * 3 + c3) * 128 + p] = True
    for l in range(4):
        W = inp["ffn_w_in"][l]
        wi = np.zeros((1024, 768), f)
        wi[:, 0:FOWN] = W[:, FOWN * r:FOWN * r + FOWN]
        wi[:, 384:384 + FOWN] = W[:, DFF + FOWN * r:DFF + FOWN * r + FOWN]
        fw.append(_km(wi))
        cw = np.zeros((3, 768), f)
        cw[:, 0:FOWN] = inp["ffn_conv_w"][l][:, FOWN * r:FOWN * r + FOWN]
        cw[:, 384:384 + FOWN] = inp["ffn_conv_w"][l][:, DFF + FOWN * r:DFF + FOWN * r + FOWN]
        fcv.append(cw.T.reshape(6, 128, 3).transpose(1, 0, 2))
        cb = np.zeros((768,), f)
        cb[0:FOWN] = inp["ffn_conv_b"][l][FOWN * r:FOWN * r + FOWN]
        cb[384:384 + FOWN] = inp["ffn_conv_b"][l][DFF + FOWN * r:DFF + FOWN * r + FOWN]
        fcb.append(cb.reshape(6, 128).T)
        wo = np.where(rvalid[:, None], inp["ffn_w_out"][l][rows][:, 128 * r:128 * r + 128], 0.0).astype(f)
        fwo.append(_km(wo))
    o["fw_in"] = np.stack(fw)
    o["fconv"] = np.stack(fcv, axis=1)
    o["fcb"] = np.stack(fcb, axis=1)
    o["fw_out"] = np.stack(fwo)
    o["xs_own"] = np.ascontiguousarray(inp["x_sample"].reshape(512, D)[:, 128 * r:128 * r + 128].T)
    o["xp_own"] = np.ascontiguousarray(inp["x_prompt"][0][:, 128 * r:128 * r + 128].T)
    sg, sgc = [], []
    for nm in ("l0", "l3"):
        sg.append(inp["state_%s_gdn_S" % nm][:, h])
        cv = inp["state_%s_gdn_conv" % nm]
        sgc.append(np.stack([cv[:, :, 128 * h + 1024 * w:128 * h + 1024 * w + 128] for w in range(3)], axis=0)
                   .transpose(3, 1, 0, 2))
    o["sgS"] = np.stack(sg)
    o["sgconv"] = np.stack(sgc)
    kc = inp["cache_l1_sb_k"][:, :, 2 * r:2 * r + 2, :]
    o["kcache"] = np.ascontiguousarray(kc.reshape(16, PAST, 128).transpose(0, 2, 1))
    vc = inp["cache_l1_sb_v"][:, :, 2 * r:2 * r + 2, :].reshape(16, 32, 128, 128)
    o["vcache"] = np.ascontiguousarray(vc.transpose(0, 2, 1, 3))
    R0 = inp["state_l2_ret"][:, hh][:, :, perm]
    o["sR"] = np.ascontiguousarray(R0.reshape(16, 2, 128, 512).transpose(0, 2, 1, 3))
    fc = inp["state_ffn_conv"]
    sf = np.zeros((4, 128, 16, 6, 2), f)
    for l in range(4):
        pad = np.zeros((16, 2, 768), f)
        pad[:, :, 0:FOWN] = fc[l][:, :, FOWN * r:FOWN * r + FOWN]
        pad[:, :, 384:384 + FOWN] = fc[l][:, :, DFF + FOWN * r:DFF + FOWN * r + FOWN]
        sf[l] = pad.reshape(16, 2, 6, 128).transpose(3, 0, 2, 1)
    o["sfconv"] = sf
    return {k: np.ascontiguousarray(v, dtype=f) for k, v in o.items()}


IN_SHAPES = None


class StopStep(Exception):
    pass


def dbg_stop(name):
    if DBG.get("stop") == name:
        DBG["cnt"] = DBG.get("cnt", 0) + 1
        if DBG["cnt"] >= DBG.get("occ", 1):
            raise StopStep()


def build(shapes, NSTEP=SEQ // NTP, NL=4, DO_SAMPLE=True, NS_RUN=NSMP):
    nc = bass.Bass("TRN2", target_bir_lowering=False)
    din = {k: nc.dram_tensor(k, list(s), F32, kind="ExternalInput") for k, s in shapes.items()}
    NTOK = NSTEP * NTP
    outs = {
        "y_own": [128, SEQ + 512], "ok": [128, SEQ + 512], "ov": [128, SEQ + 512],
        "oS": [2, 17, 128, 128], "oconv": [2, 17, 128, 3, 3], "oR": [17, 128, 2, 512], "ofc": [4, 17, 128, 6, 2],
    }
    dout = {k: nc.dram_tensor(k, s, F32, kind="ExternalOutput") for k, s in outs.items()}
    tout = {k: T(v, k) for k, v in dout.items()}
    st = ExitStack()
    with st:
        P = Prog(nc, st)
        PS = [P.ps([128, 512], F32) for _ in range(6)]
        PB = [P.ps([128, 1024], BF16) for _ in range(2)]
        psi = [0]

        def bank():
            psi[0] = (psi[0] + 1) % 4
            return PS[psi[0]]
        PLONG = [PS[4], PS[5]]
        pbi = [0]

        def bbank():
            pbi[0] = (pbi[0] + 1) % 2
            return PB[pbi[0]]

        WST = [P.sb([128, 2048], F32) for _ in range(1)]
        wsi = [0]

        def cload(name, shape, dt=F32, src=None):
            full = tuple(slice(None) for _ in shape)
            src = src if src is not None else din[name].ap()
            if dt == F32:
                t = P.sb(shape, F32)
                P.dma("sp", t[full], src, writes=[t])
                return t
            nel = int(np.prod(shape[1:]))
            stg = WST[0]
            wsi[0] += 1
            sv = stg[0:shape[0], 0:nel]
            if len(shape) == 3:
                sv = sv.rearrange("p (a b) -> p a b", a=shape[1])
            P.dma("sp", sv, src, writes=[stg])
            tb = P.sb(shape, dt)
            P.op("dve", lambda e: e.tensor_copy(out=tb[full], in_=sv), [stg], [tb])
            return tb
        ident_f = cload("ident", [128, 128])
        ident_b = cload("ident", [128, 128], BF16)
        blk64_b = cload("blk64", [128, 128], BF16)
        triI = cload("triI", [128, 128])
        triS = cload("triS", [128, 128])
        negU_b = cload("negU", [128, 128], BF16)
        maskP = cload("maskP", [128, NTP // 128, NTP], BF16)
        lvU = {64: cload("lvU64", [128, 6, NTP], BF16), 32: cload("lvU32", [128, 5, 32], BF16)}
        lvL = {64: cload("lvL64", [128, 6, NTP], BF16), 32: cload("lvL32", [128, 5, 32], BF16)}
        idrep = {64: cload("idrep64", [128, NTP]), 32: cload("idrep32", [128, 32])}
        reset64 = cload("reset64", [1, 512])
        reset32 = cload("reset32", [1, 32])
        invf = cload("invf", [128, 1])
        iota = cload("iota", [128, 512])
        rconst = cload("rconst", [128, 644])
        ones_b = P.sb([128, 128], BF16)
        P.op("dve", lambda e: e.memset(ones_b[:, :], 1.0), [], [ones_b])
        epsc = P.sb([128, 5], F32)
        for i_, v_ in enumerate((D * EPS, 128 * EPS, EPS, 64 * EPS, 512 * EPS)):
            P.op("dve", lambda e, i_=i_, v_=v_: e.memset(epsc[:, i_:i_ + 1], v_), [], [epsc])

        def rsqrt(out_t, out_ap, in_t, in_ap, ei, np_=128):
            P.op("act", lambda e: e.activation(out=out_ap, in_=in_ap, func=AF.Sqrt, bias=epsc[0:np_, ei:ei + 1], scale=1.0), [in_t, epsc], [out_t])
            P.op("dve", lambda e: e.reciprocal(out=out_ap, in_=out_ap), [out_t], [out_t])
        ones_f = P.sb([128, 128], F32)
        P.op("dve", lambda e: e.memset(ones_f[:, :], 1.0), [], [ones_f])
        gconv = cload("gconv", [128, 2, 3, 4])
        gsc = cload("gsc", [128, 2, 2])
        gng = cload("gng", [128, 2])
        sng = cload("sng", [128, 2])
        rng_t = cload("rng", [128, 4])
        fconv = cload("fconv", [128, 4, 6, 3])
        fcb = cload("fcb", [128, 4, 6])
        normg = cload("normg", [128, 4, 2, 8])
        adab = cload("adab", [128, 4, 32])
        adab_own = cload("adab_own", [128, 8])
        gab_b = [cload("gw_ab", [128, 8, 2], BF16, src=din["gw_ab"].ap()[j]) for j in range(2)]
        negA = P.sb([128, 2], F32)
        for j in range(2):
            P.op("act", lambda e, j=j: e.activation(out=negA[:, j:j + 1], in_=gsc[:, j, 0:1], func=AF.Exp), [gsc], [negA])
        P.op("dve", lambda e: e.tensor_scalar(out=negA[:, :], in0=negA[:, :], scalar1=-1.0, scalar2=None, op0=ALU.mult), [negA], [negA])

        WB = [P.sb([128, 2048], BF16) for _ in range(2)]
        wbi = [0]

        def load_w(src_ap, kc, ncol):
            assert kc * ncol <= 2048
            s = WST[0]
            wsi[0] += 1
            b = WB[wbi[0] % 2]
            wbi[0] += 1
            n = kc * ncol
            P.dma("sp", s[:, 0:n].rearrange("p (k c) -> p k c", k=kc), src_ap, writes=[s])
            P.op("pool", lambda e: e.tensor_copy(out=b[:, 0:n], in_=s[:, 0:n]), [s], [b])
            return b, (lambda k, c0, c1, b=b, ncol=ncol: b[:, k * ncol + c0:k * ncol + c1])

        MOD = P.sb([128, 4, 4, 8, 17], F32)
        GT = P.sb([128, 4, 2, 17], F32)
        cT = cload("cT", [128, 8, 17])
        scb = P.sb([128, 8, 17], BF16)
        P.op("act", lambda e: e.activation(out=scb[:, :, :], in_=cT[:, :, :], func=AF.Silu), [cT], [scb])
        for l in range(NL):
            for g in range(16):
                wb, wv = load_w(din["adaw"].ap()[l][:, :, g * 256:(g + 1) * 256], 8, 256)
                pb = bank()
                for fcx in range(2):
                    for k in range(8):
                        P.op("pe", lambda e, k=k, fcx=fcx, wv=wv, pb=pb: e.matmul(pb[:, fcx * 17:(fcx + 1) * 17], lhsT=wv(k, fcx * 128, fcx * 128 + 128), rhs=scb[:, k, :], start=(k == 0), stop=(k == 7)), [wb, scb], [pb])
                for fcx in range(2):
                    cc_ = g * 2 + fcx
                    kind, ch = cc_ // 8, cc_ % 8
                    if kind in (0, 2):
                        dst = MOD[:, l, 1 if kind == 0 else 3, ch, :]
                        P.op("dve", lambda e, dst=dst, pb=pb, fcx=fcx, l=l, cc_=cc_: e.tensor_scalar(out=dst, in0=pb[:, fcx * 17:(fcx + 1) * 17], scalar1=adab[:, l, cc_:cc_ + 1], scalar2=None, op0=ALU.add), [pb, adab], [MOD])
                    else:
                        dst = MOD[:, l, 0 if kind == 1 else 2, ch, :]
                        gi = 0 if kind == 1 else 1
                        P.op("dve", lambda e, dst=dst, pb=pb, fcx=fcx, l=l, cc_=cc_: e.tensor_scalar(out=dst, in0=pb[:, fcx * 17:(fcx + 1) * 17], scalar1=adab[:, l, cc_:cc_ + 1], scalar2=1.0, op0=ALU.add, op1=ALU.add), [pb, adab], [MOD])
                        P.op("dve", lambda e, dst=dst, l=l, gi=gi, ch=ch: e.tensor_scalar(out=dst, in0=dst, scalar1=normg[:, l, gi, ch:ch + 1], scalar2=None, op0=ALU.mult), [MOD, normg], [MOD])
        for g in range(4):
            wb, wv = load_w(din["adaw_own"].ap()[:, :, g * 256:(g + 1) * 256], 8, 256)
            pb = bank()
            for fcx in range(2):
                for k in range(8):
                    P.op("pe", lambda e, k=k, fcx=fcx, wv=wv, pb=pb: e.matmul(pb[:, fcx * 17:(fcx + 1) * 17], lhsT=wv(k, fcx * 128, fcx * 128 + 128), rhs=scb[:, k, :], start=(k == 0), stop=(k == 7)), [wb, scb], [pb])
            for fcx in range(2):
                i_ = g * 2 + fcx
                l, w2 = i_ // 2, i_ % 2
                P.op("dve", lambda e, pb=pb, fcx=fcx, l=l, w2=w2, i_=i_: e.tensor_scalar(out=GT[:, l, w2, :], in0=pb[:, fcx * 17:(fcx + 1) * 17], scalar1=adab_own[:, i_:i_ + 1], scalar2=None, op0=ALU.add), [pb, adab_own], [GT])

        agc = [0]
        ag_bufs = {}

        def exchange(src_t, src_ap, nch, nt, dt, dst_t, dst_ap3):
            key = (nch, nt, dt, agc[0] % 2)
            agc[0] += 1
            if key not in ag_bufs:
                i = len(ag_bufs)
                a = nc.dram_tensor("agi%d" % i, [nch * 128, nt], dt)
                b = nc.dram_tensor("ago%d" % i, [8 * nch * 128, nt], dt)
                ag_bufs[key] = (a, b, T(a), T(b))
                DBG.setdefault("cc_bufs", {})[id(ag_bufs[key][2])] = (a, b)
            a, b, ta, tb = ag_bufs[key]
            P.dma("act", a.ap().rearrange("(c p) n -> p c n", p=128), src_ap, reads=[src_t], writes=[ta])
            P.cc([ta], [tb], [a.ap().opt()], [b.ap().opt()])
            P.dma("sp", dst_ap3, b.ap().rearrange("(c p) n -> p c n", p=128), reads=[tb], writes=[dst_t])

        class Ctx:
            pass

        def make_ctx(NT, C, tag):
            c = Ctx()
            c.NT, c.C, c.n = NT, C, NT // C
            c.xfull = P.sb([128, 8, NT], F32)
            c.xown = P.sb([128, NT], F32)
            c.hb = P.sb([128, 8, NT], BF16)
            c.rstd = P.sb([128, NT], F32)
            c.f = [P.sb([128, 4, NT], F32) for _ in range(5)]
            c.fpad = P.sb([128, 6, NT + 4], F32)
            c.b = [P.sb([128, 4, NT], BF16) for _ in range(4)]
            c.oall = P.sb([128, 24, NT], BF16)
            c.sm = [P.sb([128, max(c.n * 128, 64)], F32) for _ in range(4)]
            c.smb = [P.sb([128, max(c.n * 128, 128)], BF16) for _ in range(7)]
            c.row = [P.sb([1, NT], F32) for _ in range(4)]
            c.S = [P.sb([128, 128], F32) for _ in range(2)]
            c.Sb = [P.sb([128, 128], BF16) for _ in range(2)]
            c.R = P.sb([128, 2, 512], F32)
            c.Rb = P.sb([128, 2, 512], BF16)
            c.ni = P.sb([128, NT], mybir.dt.int32)
            c.rk = P.sb([128, c.n * 256], BF16)
            c.rv = P.sb([128, c.n * 512], BF16)
            nlev_ = 6 if C == 64 else 5
            c.lev = [[P.sb([128, NT], F32) for _ in range(2)], [P.sb([128, NT], F32) for _ in range(2)]] + [P.sb([128, NT], F32) for _ in range(4)]
            c.att = [dict(e=P.sb([128, NT], F32), sp=P.sb([128, NT], BF16), t=P.sb([128, NT], F32), w=P.sb([128, NT], BF16), Cb=P.sb([128, NT], F32)) for _ in range(2)]
            c.gtail = [P.sb([128, 3, 3], F32) for _ in range(2)]
            c.ftail = [P.sb([128, 6, 2], F32) for _ in range(4)]
            return c

        def norm_mod(c, l, which, seq):
            NT = c.NT
            pb = bank()
            sqb = c.b[0]
            for half in range(2):
                P.op("act", lambda e, half=half: e.activation(out=sqb[:, :, :], in_=c.xfull[:, half * 4:half * 4 + 4, :], func=AF.Square), [c.xfull], [sqb])
                for k in range(4):
                    P.op("pe", lambda e, k=k, half=half, pb=pb: e.matmul(pb[:, 0:NT], lhsT=ones_b[:, :], rhs=sqb[:, k, :], start=(half == 0 and k == 0), stop=(half == 1 and k == 3)), [ones_b, sqb], [pb])
            rsqrt(c.rstd, c.rstd[:, :], pb, pb[:, 0:NT], 0)
            tmp = c.f[0]
            ai, bi = (0, 1) if which == 0 else (2, 3)
            for k in range(8):
                P.op("dve", lambda e, k=k: e.scalar_tensor_tensor(out=tmp[:, k % 4, :], in0=c.xfull[:, k, :], scalar=float(D) ** 0.5, in1=c.rstd[:, :], op0=ALU.mult, op1=ALU.mult), [c.xfull, c.rstd], [tmp])
                P.op("dve", lambda e, k=k: e.tensor_scalar(out=c.hb[:, k, :], in0=tmp[:, k % 4, :], scalar1=MOD[:, l, ai, k, seq:seq + 1], scalar2=MOD[:, l, bi, k, seq:seq + 1], op0=ALU.mult, op1=ALU.add), [tmp, MOD], [c.hb])

        def proj(c, wsrc, ncols, evac, kc=8, rhs_t=None, rhs_fn=None, group=256):
            NT = c.NT
            rhs_t = rhs_t or c.hb
            rhs_fn = rhs_fn or (lambda k: c.hb[:, k, :])
            fc0 = 0
            for g0 in range(0, ncols, group):
                gw = min(group, ncols - g0)
                wb, wv = load_w(wsrc[:, :, g0:g0 + gw], kc, gw)
                for fcx in range(gw // 128):
                    pb = bank()
                    for k in range(kc):
                        P.op("pe", lambda e, k=k, fcx=fcx, wv=wv, pb=pb: e.matmul(pb[:, 0:NT], lhsT=wv(k, fcx * 128, fcx * 128 + 128), rhs=rhs_fn(k), start=(k == 0), stop=(k == kc - 1)), [wb, rhs_t], [pb])
                    evac(fc0, pb)
                    fc0 += 1

        def out_proj_update(c, l, w2, seq, wsrc, kc, col0):
            NT = c.NT
            pb = bank()
            for g0 in range(0, kc, 16):
                gk = min(16, kc - g0)
                wb, wv = load_w(wsrc[:, g0:g0 + gk, :], gk, 128)
                for k in range(gk):
                    P.op("pe", lambda e, k=k, g0=g0, wv=wv, pb=pb: e.matmul(pb[:, 0:NT], lhsT=wv(k, 0, 128), rhs=c.oall[:, g0 + k, :], start=(g0 + k == 0), stop=(g0 + k == kc - 1)), [wb, c.oall], [pb])
            P.op("dve", lambda e, pb=pb: e.scalar_tensor_tensor(out=c.xown[:, :], in0=pb[:, 0:NT], scalar=GT[:, l, w2, seq:seq + 1], in1=c.xown[:, :], op0=ALU.mult, op1=ALU.add), [pb, GT, c.xown], [c.xown])

        def xchg_x(c):
            exchange(c.xown, c.xown[:, :].rearrange("p (o n) -> p o n", o=1), 1, c.NT, F32, c.xfull, c.xfull[:, :, :])

        def ffn(c, l, seq, first):
            NT = c.NT
            norm_mod(c, l, 1, seq)
            up = c.fpad
            if first:
                pass
            P.op("pool", lambda e: e.tensor_copy(out=up[:, :, 0:2], in_=c.ftail[l][:, :, :]), [c.ftail[l]], [up])

            def evac(fcx, pb):
                P.op("act", lambda e: e.activation(out=up[:, fcx, 2:2 + NT], in_=pb[:, 0:NT], func=AF.Copy), [pb], [up])
            proj(c, din["fw_in"].ap()[l], 768, evac, group=256)
            P.op("pool", lambda e: e.tensor_copy(out=c.ftail[l][:, :, :], in_=up[:, :, NT:NT + 2]), [up], [c.ftail[l]])
            u = c.f[1]
            u2 = c.f[2]
            act = c.b[1]
            for fcx in range(6):
                dst_t = u if fcx < 4 else u2
                dst = dst_t[:, fcx % 4, :]
                P.op("dve", lambda e, fcx=fcx, dst=dst: e.tensor_scalar(out=dst, in0=up[:, fcx, 0:NT], scalar1=fconv[:, l, fcx, 0:1], scalar2=fcb[:, l, fcx:fcx + 1], op0=ALU.mult, op1=ALU.add), [up, fconv, fcb], [dst_t])
                for tap in (1, 2):
                    P.op("dve", lambda e, fcx=fcx, dst=dst, tap=tap: e.scalar_tensor_tensor(out=dst, in0=up[:, fcx, tap:tap + NT], scalar=fconv[:, l, fcx, tap:tap + 1], in1=dst, op0=ALU.mult, op1=ALU.add), [up, fconv, dst_t], [dst_t])
            dbg_stop("ffn_conv")
            sg = c.f[3]
            for fcx in range(3):
                P.op("act", lambda e, fcx=fcx: e.activation(out=sg[:, fcx, :], in_=u[:, fcx, :], func=AF.Silu), [u], [sg])
                vsrc_t = u if fcx + 3 < 4 else u2
                P.op("dve", lambda e, fcx=fcx, vsrc_t=vsrc_t: e.tensor_tensor(out=act[:, fcx, :], in0=sg[:, fcx, :], in1=vsrc_t[:, (fcx + 3) % 4, :], op=ALU.mult), [sg, vsrc_t], [act])
            dbg_stop("ffn_act")
            exchange(act, act[:, 0:3, :], 3, NT, BF16, c.oall, c.oall[:, 0:24, :])
            yield
            dbg_stop("ffn_ag")
            out_proj_update(c, l, 1, seq, din["fw_out"].ap()[l], 24, 0)
            dbg_stop("ffn_out")
            xchg_x(c)
            yield

        def gdn(c, l, j, seq):
            NT, C, n = c.NT, c.C, c.n
            norm_mod(c, l, 0, seq)
            xp = c.fpad
            P.op("pool", lambda e: e.tensor_copy(out=xp[:, 0:3, 0:3], in_=c.gtail[j][:, :, :]), [c.gtail[j]], [xp])
            zt = c.f[1]

            def evac(fcx, pb):
                if fcx < 3:
                    P.op("act", lambda e: e.activation(out=xp[:, fcx, 3:3 + NT], in_=pb[:, 0:NT], func=AF.Copy), [pb], [xp])
                else:
                    P.op("act", lambda e: e.activation(out=zt[:, 0, :], in_=pb[:, 0:NT], func=AF.Silu), [pb], [zt])
            proj(c, din["gw_in"].ap()[j], 512, evac)
            dbg_stop("gdn_proj")
            pa, pa2 = bank(), bank()
            for w2, pbx in ((0, pa), (1, pa2)):
                for k in range(8):
                    P.op("pe", lambda e, k=k, w2=w2, pbx=pbx: e.matmul(pbx[0:1, 0:NT], lhsT=gab_b[j][:, k, w2:w2 + 1], rhs=c.hb[:, k, :], start=(k == 0), stop=(k == 7)), [gab_b[j], c.hb], [pbx])
            a_ap, b_ap, a_t, b_t = pa[0:1, 0:NT], pa2[0:1, 0:NT], pa, pa2
            P.op("pool", lambda e: e.tensor_copy(out=c.gtail[j][:, :, :], in_=xp[:, 0:3, NT:NT + 3]), [xp], [c.gtail[j]])
            g_row, be_row, G_row, t_row = c.row[0], c.row[1], c.row[2], c.row[3]
            P.op("act", lambda e: e.activation(out=t_row[:, :], in_=a_ap, func=AF.Exp, bias=gsc[0:1, j, 1:2], scale=1.0), [a_t, gsc], [t_row])
            P.op("act", lambda e: e.activation(out=t_row[:, :], in_=t_row[:, :], func=AF.Ln, bias=ones_f[0:1, 0:1], scale=1.0), [t_row, ones_f], [t_row])
            P.op("dve", lambda e: e.tensor_scalar(out=g_row[:, :], in0=t_row[:, :], scalar1=negA[0:1, j:j + 1], scalar2=None, op0=ALU.mult), [t_row, negA], [g_row])
            P.op("act", lambda e: e.activation(out=be_row[:, :], in_=b_ap, func=AF.Exp, scale=-1.0), [b_t], [be_row])
            P.op("dve", lambda e: e.tensor_scalar(out=be_row[:, :], in0=be_row[:, :], scalar1=1.0, scalar2=None, op0=ALU.add), [be_row], [be_row])
            P.op("dve", lambda e: e.reciprocal(out=be_row[:, :], in_=be_row[:, :]), [be_row], [be_row])
            rs = reset64 if C == 64 else reset32
            P.op("dve", lambda e: e.tensor_tensor_scan(out=G_row[:, :], data0=rs[0:1, 0:NT], data1=g_row[:, :], initial=0.0, op0=ALU.mult, op1=ALU.add), [rs, g_row], [G_row])
            GB, EG, BB = c.f[2], c.f[2], c.f[2]
            pg = bank()
            P.op("pe", lambda e: e.matmul(pg[:, 0:NT], lhsT=ones_f[0:1, :], rhs=G_row[:, :], start=True, stop=True), [ones_f, G_row], [pg])
            P.op("act", lambda e: e.activation(out=GB[:, 0, :], in_=pg[:, 0:NT], func=AF.Copy), [pg], [GB])
            P.op("act", lambda e: e.activation(out=EG[:, 1, :], in_=pg[:, 0:NT], func=AF.Exp), [pg], [EG])
            pbb = bank()
            P.op("pe", lambda e: e.matmul(pbb[:, 0:NT], lhsT=ones_f[0:1, :], rhs=be_row[:, :], start=True, stop=True), [ones_f, be_row], [pbb])
            P.op("act", lambda e: e.activation(out=BB[:, 2, :], in_=pbb[:, 0:NT], func=AF.Copy), [pbb], [BB])
            pc = bank()
            for m in range(n):
                P.op("pe", lambda e, m=m: e.matmul(pc[0:C, m:m + 1], lhsT=G_row[0:1, m * C:(m + 1) * C], rhs=ones_f[0:1, 0:1], start=True, stop=True), [G_row, ones_f], [pc])
                P.op("pe", lambda e, m=m: e.matmul(pc[0:C, 16 + m:17 + m], lhsT=be_row[0:1, m * C:(m + 1) * C], rhs=ones_f[0:1, 0:1], start=True, stop=True), [be_row, ones_f], [pc])
            cols = c.sm[0]
            P.op("dve", lambda e: e.tensor_copy(out=cols[0:C, 0:32], in_=pc[0:C, 0:32]), [pc], [cols])
            P.op("act", lambda e: e.activation(out=cols[0:C, 32:32 + n], in_=cols[0:C, 0:n], func=AF.Exp), [cols], [cols])
            P.op("dve", lambda e: e.tensor_tensor(out=cols[0:C, 32:32 + n], in0=cols[0:C, 32:32 + n], in1=cols[0:C, 16:16 + n], op=ALU.mult), [cols], [cols])
            for m in range(n):
                P.op("act", lambda e, m=m: e.activation(out=cols[0:C, 48 + m:49 + m], in_=cols[0:C, m:m + 1], func=AF.Exp, bias=GB[0:C, 0, (m + 1) * C - 1:(m + 1) * C], scale=-1.0), [cols, GB], [cols])
            dbg_stop("gdn_gates")
            y = c.f[3]
            for w in range(3):
                P.op("dve", lambda e, w=w: e.tensor_scalar(out=y[:, w, :], in0=xp[:, w, 0:NT], scalar1=gconv[:, j, w, 0:1], scalar2=None, op0=ALU.mult), [xp, gconv], [y])
                for tap in (1, 2, 3):
                    P.op("dve", lambda e, w=w, tap=tap: e.scalar_tensor_tensor(out=y[:, w, :], in0=xp[:, w, tap:tap + NT], scalar=gconv[:, j, w, tap:tap + 1], in1=y[:, w, :], op0=ALU.mult, op1=ALU.add), [xp, gconv, y], [y])
                P.op("act", lambda e, w=w: e.activation(out=y[:, w, :], in_=y[:, w, :], func=AF.Silu), [y], [y])
            sqb = c.b[0]
            P.op("act", lambda e: e.activation(out=sqb[:, 0:2, :], in_=y[:, 0:2, :], func=AF.Square), [y], [sqb])
            qkn = c.b[1]
            for w in range(2):
                pn = bank()
                P.op("pe", lambda e, w=w, pn=pn: e.matmul(pn[:, 0:NT], lhsT=ones_b[:, :], rhs=sqb[:, w, :], start=True, stop=True), [ones_b, sqb], [pn])
                rn = c.f[4]
                rsqrt(rn, rn[:, w, :], pn, pn[:, 0:NT], 2)
                P.op("dve", lambda e, w=w: e.scalar_tensor_tensor(out=qkn[:, w, :], in0=y[:, w, :], scalar=(128.0 ** -0.5 if w == 0 else 1.0), in1=rn[:, w, :], op0=ALU.mult, op1=ALU.mult), [y, rn], [qkn])
            P.op("pool", lambda e: e.tensor_copy(out=qkn[:, 2, :], in_=y[:, 2, :]), [y], [qkn])
            P.op("dve", lambda e: e.tensor_tensor(out=qkn[:, 3, :], in0=qkn[:, 0, :], in1=EG[:, 1, :], op=ALU.mult), [qkn, EG], [qkn])
            dbg_stop("gdn_norm")
            vtok = c.smb[1]
            kb_, kbg, kd = c.smb[0], c.smb[6], c.smb[4]
            for src_w, which in ((1, "k"), (2, "v")):
                if which == "v":
                    dbg_stop("gdn_tok_k")
                for m0 in range(0, n, 8):
                    pt = bbank()
                    for m in range(m0, min(n, m0 + 8)):
                        P.op("pe", lambda e, m=m, m0=m0, pt=pt, src_w=src_w: e.transpose(pt[0:C, (m - m0) * 128:(m - m0 + 1) * 128], qkn[:, src_w, m * C:(m + 1) * C], ident_b[:, :]), [qkn, ident_b], [pt])
                    dbg_stop("gdn_tok_tr")
                    raw = c.smb[2] if which == "k" else c.smb[3]
                    nm_ = min(n, m0 + 8) - m0
                    P.op("dve", lambda e, pt=pt, raw=raw, nm_=nm_: e.tensor_copy(out=raw[0:C, 0:nm_ * 128], in_=pt[0:C, 0:nm_ * 128]), [pt], [raw])
                    for m in range(m0, min(n, m0 + 8)):
                        src = raw[0:C, (m - m0) * 128:(m - m0 + 1) * 128]
                        sl = slice(m * 128, (m + 1) * 128)
                        if which == "k":
                            P.op("dve", lambda e, src=src, sl=sl, m=m: e.tensor_scalar(out=kb_[0:C, sl], in0=src, scalar1=cols[0:C, 16 + m:17 + m], scalar2=None, op0=ALU.mult), [raw, cols], [kb_])
                            P.op("pool", lambda e, src=src, sl=sl, m=m: e.tensor_scalar(out=kbg[0:C, sl], in0=src, scalar1=cols[0:C, 32 + m:33 + m], scalar2=None, op0=ALU.mult), [raw, cols], [kbg])
                            P.op("dve", lambda e, src=src, sl=sl, m=m: e.tensor_scalar(out=kd[0:C, sl], in0=src, scalar1=cols[0:C, 48 + m:49 + m], scalar2=None, op0=ALU.mult), [raw, cols], [kd])
                        else:
                            P.op("pool", lambda e, src=src, sl=sl, m=m: e.tensor_scalar(out=vtok[0:C, sl], in0=src, scalar1=cols[0:C, 16 + m:17 + m], scalar2=None, op0=ALU.mult), [raw, cols], [vtok])
            dbg_stop("gdn_tok")
            pk, pq = bank(), bank()
            for m in range(n):
                cs = slice(m * C, (m + 1) * C)
                P.op("pe", lambda e, cs=cs, pk=pk: e.matmul(pk[0:C, cs], lhsT=qkn[:, 1, cs], rhs=qkn[:, 1, cs], start=True, stop=True), [qkn], [pk])
                P.op("pe", lambda e, cs=cs, pq=pq: e.matmul(pq[0:C, cs], lhsT=qkn[:, 1, cs], rhs=qkn[:, 0, cs], start=True, stop=True), [qkn], [pq])
            Dm = c.sm[1]
            tI = triI[0:C, 0:C]
            tS = triS[0:C, 0:C]
            for m in range(n):
                cs = slice(m * C, (m + 1) * C)
                P.op("dve", lambda e, cs=cs, m=m: e.tensor_scalar(out=Dm[0:C, cs], in0=GB[0:C, 0, cs], scalar1=cols[0:C, m:m + 1], scalar2=0.0, op0=ALU.subtract, op1=ALU.min), [GB, cols], [Dm])
            P.op("act", lambda e: e.activation(out=Dm[0:C, 0:NT], in_=Dm[0:C, 0:NT], func=AF.Exp), [Dm], [Dm])
            for m in range(n):
                cs = slice(m * C, (m + 1) * C)
                P.op("pool", lambda e, cs=cs: e.tensor_tensor(out=Dm[0:C, cs], in0=Dm[0:C, cs], in1=tI, op=ALU.mult), [Dm, triI], [Dm])
            qkT = c.smb[5]
            P.op("dve", lambda e: e.tensor_tensor(out=qkT[0:C, 0:NT], in0=pq[0:C, 0:NT], in1=Dm[0:C, 0:NT], op=ALU.mult), [pq, Dm], [qkT])
            Nf = c.sm[2]
            P.op("dve", lambda e: e.tensor_tensor(out=Nf[0:C, 0:NT], in0=pk[0:C, 0:NT], in1=Dm[0:C, 0:NT], op=ALU.mult), [pk, Dm], [Nf])
            P.op("dve", lambda e: e.tensor_tensor(out=Nf[0:C, 0:NT], in0=Nf[0:C, 0:NT], in1=BB[0:C, 2, :], op=ALU.mult), [Nf, BB], [Nf])
            for m in range(n):
                cs = slice(m * C, (m + 1) * C)
                P.op("pool", lambda e, cs=cs: e.tensor_tensor(out=Nf[0:C, cs], in0=Nf[0:C, cs], in1=tS, op=ALU.mult), [Nf, triS], [Nf])
            Lf = c.sm[3]
            ptf = bank()
            for m in range(n):
                cs = slice(m * C, (m + 1) * C)
                P.op("pe", lambda e, cs=cs, ptf=ptf: e.transpose(ptf[0:C, cs], Nf[0:C, cs], ident_f[0:C, 0:C]), [Nf, ident_f], [ptf])
            P.op("dve", lambda e, ptf=ptf: e.tensor_copy(out=Lf[0:C, 0:NT], in_=ptf[0:C, 0:NT]), [ptf], [Lf])
            dbg_stop("gdn_NL")
            nlev = 6 if C == 64 else 5
            Bk2, BkT2 = c.lev[0], c.lev[1]
            X, XT, Yf, Ypf = c.lev[2], c.lev[3], c.lev[4], c.lev[5]
            P.op("dve", lambda e: e.tensor_copy(out=X[0:C, :], in_=idrep[C][0:C, 0:NT]), [idrep[C]], [X])
            P.op("dve", lambda e: e.tensor_copy(out=XT[0:C, :], in_=idrep[C][0:C, 0:NT]), [idrep[C]], [XT])
            for k in range(nlev):
                Bk, BkT = Bk2[k % 2], BkT2[k % 2]
                P.op("pool", lambda e, k=k, Bk=Bk: e.tensor_tensor(out=Bk[0:C, :], in0=Nf[0:C, 0:NT], in1=lvU[C][0:C, k, 0:NT], op=ALU.mult), [Nf, lvU[C]], [Bk])
                P.op("pool", lambda e, k=k, BkT=BkT: e.tensor_tensor(out=BkT[0:C, :], in0=Lf[0:C, 0:NT], in1=lvL[C][0:C, k, 0:NT], op=ALU.mult), [Lf, lvL[C]], [BkT])
                py, pyp = bank(), bank()
                for m in range(n):
                    cs = slice(m * C, (m + 1) * C)
                    P.op("pe", lambda e, cs=cs, py=py, BkT=BkT: e.matmul(py[0:C, cs], lhsT=BkT[0:C, cs], rhs=X[0:C, cs], start=True, stop=True), [BkT, X], [py])
                    P.op("pe", lambda e, cs=cs, pyp=pyp, Bk=Bk: e.matmul(pyp[0:C, cs], lhsT=Bk[0:C, cs], rhs=XT[0:C, cs], start=True, stop=True), [Bk, XT], [pyp])
                P.op("act", lambda e, py=py: e.activation(out=Yf[0:C, :], in_=py[0:C, 0:NT], func=AF.Copy), [py], [Yf])
                P.op("dve", lambda e, pyp=pyp: e.tensor_copy(out=Ypf[0:C, :], in_=pyp[0:C, 0:NT]), [pyp], [Ypf])
                pz_, pzp = bank(), bank()
                for m in range(n):
                    cs = slice(m * C, (m + 1) * C)
                    P.op("pe", lambda e, cs=cs, pz_=pz_: e.matmul(pz_[0:C, cs], lhsT=XT[0:C, cs], rhs=Yf[0:C, cs], start=True, stop=True), [XT, Yf], [pz_])
                    P.op("pe", lambda e, cs=cs, pzp=pzp: e.matmul(pzp[0:C, cs], lhsT=X[0:C, cs], rhs=Ypf[0:C, cs], start=True, stop=True), [X, Ypf], [pzp])
                P.op("dve", lambda e, pz_=pz_: e.tensor_tensor(out=X[0:C, :], in0=X[0:C, :], in1=pz_[0:C, 0:NT], op=ALU.subtract), [X, pz_], [X])
                P.op("dve", lambda e, pzp=pzp: e.tensor_tensor(out=XT[0:C, :], in0=XT[0:C, :], in1=pzp[0:C, 0:NT], op=ALU.subtract), [XT, pzp], [XT])
            Rb16 = P_rb16(c)
            P.op("act", lambda e: e.activation(out=Rb16[0:C, 0:NT], in_=X[0:C, :], func=AF.Copy), [X], [Rb16])
            pw = bank()
            for m in range(n):
                cs = slice(m * C, (m + 1) * C)
                P.op("pe", lambda e, cs=cs, m=m: e.matmul(pw[:, cs], lhsT=kbg[0:C, m * 128:(m + 1) * 128], rhs=Rb16[0:C, cs], start=True, stop=True), [kbg, Rb16], [pw])
            nwT = c.b[2]
            P.op("act", lambda e: e.activation(out=nwT[:, 0, :], in_=pw[:, 0:NT], func=AF.Copy, scale=-1.0), [pw], [nwT])
            dbg_stop("gdn_neumann")
            po = PLONG[0]
            S, Sb = c.S[j], c.Sb[j]
            for m in range(n):
                cs = slice(m * C, (m + 1) * C)
                ms = slice(m * 128, (m + 1) * 128)
                pv = bank()
                P.op("pe", lambda e, cs=cs, ms=ms, pv=pv: e.matmul(pv[0:C, 0:128], lhsT=Rb16[0:C, cs], rhs=vtok[0:C, ms], start=True, stop=False), [Rb16, vtok], [pv])
                P.op("pe", lambda e, cs=cs, pv=pv: e.matmul(pv[0:C, 0:128], lhsT=nwT[:, 0, cs], rhs=Sb[:, :], start=False, stop=True), [nwT, Sb], [pv])
                vnt = P_vn(c)
                vv = vnt[0:C, (m % 2) * 128:(m % 2 + 1) * 128]
                P.op("act", lambda e, pv=pv, vv=vv: e.activation(out=vv, in_=pv[0:C, 0:128], func=AF.Copy), [pv], [vnt])
                P.op("pe", lambda e, cs=cs: e.matmul(po[:, cs], lhsT=Sb[:, :], rhs=qkn[:, 3, cs], start=True, stop=False), [Sb, qkn], [po])
                P.op("pe", lambda e, cs=cs, vv=vv: e.matmul(po[:, cs], lhsT=vv, rhs=qkT[0:C, cs], start=False, stop=True), [vnt, qkT], [po])
                pS = bank()
                P.op("pe", lambda e, ms=ms, vv=vv, pS=pS: e.matmul(pS[:, 0:128], lhsT=kd[0:C, ms], rhs=vv, start=True, stop=True), [kd, vnt], [pS])
                P.op("dve", lambda e, m=m, pS=pS: e.scalar_tensor_tensor(out=S[:, :], in0=S[:, :], scalar=EG[:, 1, (m + 1) * C - 1:(m + 1) * C], in1=pS[:, 0:128], op0=ALU.mult, op1=ALU.add), [S, EG, pS], [S])
                P.op("act", lambda e: e.activation(out=Sb[:, :], in_=S[:, :], func=AF.Copy), [S], [Sb])
            dbg_stop("gdn_scan")
            ot = c.f[4]
            P.op("act", lambda e: e.activation(out=ot[:, 2, :], in_=po[:, 0:NT], func=AF.Copy), [po], [ot])
            P.op("act", lambda e: e.activation(out=sqb[:, 2, :], in_=po[:, 0:NT], func=AF.Square), [po], [sqb])
            pn = bank()
            P.op("pe", lambda e, pn=pn: e.matmul(pn[:, 0:NT], lhsT=ones_b[:, :], rhs=sqb[:, 2, :], start=True, stop=True), [ones_b, sqb], [pn])
            rsqrt(ot, ot[:, 3, :], pn, pn[:, 0:NT], 1)
            P.op("dve", lambda e: e.scalar_tensor_tensor(out=ot[:, 2, :], in0=ot[:, 2, :], scalar=128.0 ** 0.5, in1=ot[:, 3, :], op0=ALU.mult, op1=ALU.mult), [ot], [ot])
            og = c.b[0]
            P.op("dve", lambda e: e.scalar_tensor_tensor(out=og[:, 3, :], in0=ot[:, 2, :], scalar=gng[:, j:j + 1], in1=zt[:, 0, :], op0=ALU.mult, op1=ALU.mult), [ot, gng, zt], [og])
            exchange(og, og[:, 3:4, :], 1, NT, BF16, c.oall, c.oall[:, 0:8, :])
            yield
            out_proj_update(c, l, 0, seq, din["gw_out"].ap()[j], 8, 0)
            xchg_x(c)
            yield

        def P_rb16(c):
            if not hasattr(c, "_rb16"):
                c._rb16 = P.sb([128, max(c.n * 128, 64)], BF16)
            return c._rb16

        def P_nl2(c):
            if not hasattr(c, "_nl2"):
                c._nl2 = (P.sb([128, max(c.n * 128, 64)], BF16), P.sb([128, max(c.n * 128, 64)], BF16))
            return c._nl2

        def P_vn(c):
            if not hasattr(c, "_vn"):
                c._vn = P.sb([128, 256], BF16)
            return c._vn

        def step(c, seq, t0, col_out, first, last, si):
            NT = c.NT
            try:
                for l in range(NL):
                    kind, j = l % 3, l // 3
                    if kind == 0:
                        yield from gdn(c, l, j, seq)
                    elif kind == 1:
                        yield from sbmix(c, l, seq, t0, col_out, si)
                    else:
                        yield from retmix(c, l, seq, t0, si)
                    dbg_stop("mixer%d" % l)
                    yield from ffn(c, l, seq, first)
            except StopStep:
                pass
            P.dma("sp", dout["y_own"].ap()[:, col_out:col_out + NT], c.xown[:, :], reads=[c.xown], writes=[tout["y_own"]])

        khist_d = nc.dram_tensor("khist", [128, SEQ], BF16)
        vhist_d = nc.dram_tensor("vhist", [128, SEQ // 128, 128], BF16)
        khist, vhist = T(khist_d), T(vhist_d)
        KP = [P.sb([128, 512], BF16) for _ in range(2)]
        VP = [P.sb([128, 4, 128], BF16) for _ in range(2)]
        kpi = [0]

        def exchange_parts(parts, nrows, nt, dt, dst_t, dst_ap3):
            key = ("parts", nrows, nt, dt, agc[0] % 2)
            agc[0] += 1
            if key not in ag_bufs:
                i = len(ag_bufs)
                a = nc.dram_tensor("agi%d" % i, [nrows, nt], dt)
                b = nc.dram_tensor("ago%d" % i, [8 * nrows, nt], dt)
                ag_bufs[key] = (a, b, T(a), T(b))
                DBG.setdefault("cc_bufs", {})[id(ag_bufs[key][2])] = (a, b)
            a, b, ta, tb = ag_bufs[key]
            for (st_, sap, r0, r1) in parts:
                P.dma("act", a.ap()[r0:r1, :], sap, reads=[st_], writes=[ta])
            P.cc([ta], [tb], [a.ap().opt()], [b.ap().opt()])
            P.dma("sp", dst_ap3, b.ap().rearrange("(c p) n -> p c n", p=128), reads=[tb], writes=[dst_t])

        def sbmix(c, l, seq, t0, col_out, si):
            NT = c.NT
            norm_mod(c, l, 0, seq)
            qkv = c.f[1]

            def evac(fcx, pb):
                P.op("act", lambda e: e.activation(out=qkv[:, fcx, :], in_=pb[:, 0:NT], func=AF.Copy), [pb], [qkv])
            proj(c, din["sw_in"].ap(), 384, evac)
            sqb = c.b[0]
            P.op("act", lambda e: e.activation(out=sqb[:, 0:2, :], in_=qkv[:, 0:2, :], func=AF.Square), [qkv], [sqb])
            rn = c.f[4]
            qkb = c.b[1]
            kn32 = c.f[2]
            for w in range(2):
                pn = bank()
                P.op("pe", lambda e, w=w, pn=pn: e.matmul(pn[:, 0:NT], lhsT=blk64_b[:, :], rhs=sqb[:, w, :], start=True, stop=True), [blk64_b, sqb], [pn])
                rsqrt(rn, rn[:, w, :], pn, pn[:, 0:NT], 3)
                P.op("dve", lambda e, w=w: e.scalar_tensor_tensor(out=rn[:, 2 + w, :], in0=qkv[:, w, :], scalar=sng[:, w:w + 1], in1=rn[:, w, :], op0=ALU.mult, op1=ALU.mult), [qkv, sng, rn], [rn])
            P.op("act", lambda e: e.activation(out=qkb[:, 0, :], in_=rn[:, 2, :], func=AF.Copy), [rn], [qkb])
            P.op("act", lambda e: e.activation(out=kn32[:, 0, :], in_=rn[:, 3, :], func=AF.Copy, scale=8.0), [rn], [kn32])
            P.op("act", lambda e: e.activation(out=qkb[:, 1, :], in_=rn[:, 3, :], func=AF.Copy, scale=8.0), [rn], [qkb])
            P.op("pool", lambda e: e.tensor_copy(out=qkb[:, 2, :], in_=qkv[:, 2, :]), [qkv], [qkb])
            P.dma("sp", dout["ok"].ap()[:, col_out:col_out + NT], kn32[:, 0, :], reads=[kn32], writes=[tout["ok"]])
            P.dma("sp", dout["ov"].ap()[:, col_out:col_out + NT], qkv[:, 2, :], reads=[qkv], writes=[tout["ov"]])
            dbg_stop("sb_proj")
            KBc = min(128, NT)
            nbc = NT // KBc
            vcur = c.smb[0]
            pt = bbank()
            for b_ in range(nbc):
                P.op("pe", lambda e, b_=b_, pt=pt: e.transpose(pt[0:KBc, b_ * 128:(b_ + 1) * 128], qkb[:, 2, b_ * KBc:(b_ + 1) * KBc], ident_b[:, :]), [qkb, ident_b], [pt])
            P.op("dve", lambda e, pt=pt: e.tensor_copy(out=vcur[0:KBc, 0:nbc * 128], in_=pt[0:KBc, 0:nbc * 128]), [pt], [vcur])
            is_prompt = (seq == 0)
            if is_prompt:
                P.dma("sp", khist_d.ap()[:, t0:t0 + NT], qkb[:, 1, :], reads=[qkb], writes=[khist])
                P.dma("sp", vhist_d.ap()[:, t0 // 128:t0 // 128 + nbc, :], vcur[:, 0:nbc * 128].rearrange("p (b d) -> p b d", b=nbc), reads=[vcur], writes=[vhist])
                npast = t0 // 128
            else:
                npast = PAST // 128
            dbg_stop("sb_vt")
            att = c.att
            for hh in range(2):
                P.op("pool", lambda e, hh=hh: e.memset(att[hh]["Cb"][:, :], 0.0), [], [att[hh]["Cb"]])
            nblocks = nbc + npast
            done = [0]
            oacc = c.f[0]

            def block(Kt, Kap_fn, Vt, Vap_fn, KB, maskap):
                first = (done[0] == 0)
                last = (done[0] == nblocks - 1)
                for hh in range(2):
                    a = att[hh]
                    hs = slice(64 * hh, 64 * hh + 64)
                    bA, bB = PS[2 * hh], PS[2 * hh + 1]
                    pz = bA
                    P.op("pe", lambda e, pz=pz, hs=hs, bA=bA, bB=bB: e.matmul(pz[0:KB, 0:NT], lhsT=Kap_fn(hs), rhs=qkb[hs, 0, :], start=True, stop=True), [Kt, qkb], [pz])
                    dbg_stop("att1")
                    P.op("act", lambda e, pz=pz, a=a, bA=bA, bB=bB: e.activation(out=a["e"][0:KB, :], in_=pz[0:KB, 0:NT], func=AF.Exp), [pz], [a["e"]])
                    dbg_stop("att2")
                    P.op("act", lambda e, a=a, bA=bA, bB=bB: e.activation(out=a["sp"][0:KB, :], in_=a["e"][0:KB, :], func=AF.Ln, bias=ones_f[0:KB, 0:1], scale=1.0), [a["e"], ones_f], [a["sp"]])
                    dbg_stop("att3")
                    if maskap is not None:
                        P.op("pool", lambda e, a=a, bA=bA, bB=bB: e.tensor_tensor(out=a["sp"][0:KB, :], in0=a["sp"][0:KB, :], in1=maskap, op=ALU.mult), [a["sp"], maskP], [a["sp"]])
                    dbg_stop("att4")
                    P.op("pe", lambda e, a=a, bA=bA, bB=bB: e.matmul(bA[0:KB, NT:2 * NT], lhsT=negU_b[0:KB, 0:KB], rhs=a["sp"][0:KB, :], start=True, stop=True), [negU_b, a["sp"]], [bA])
                    dbg_stop("att5")
                    P.op("pe", lambda e, a=a, bA=bA, bB=bB: e.matmul(bB[:, 0:NT], lhsT=ones_b[0:KB, :], rhs=a["sp"][0:KB, :], start=True, stop=True), [ones_b, a["sp"]], [bB])
                    dbg_stop("att6")
                    P.op("dve", lambda e, pz=pz, a=a, bA=bA, bB=bB: e.tensor_tensor(out=a["t"][0:KB, :], in0=pz[0:KB, 0:NT], in1=a["Cb"][0:KB, :], op=ALU.subtract), [pz, a["Cb"]], [a["t"]])
                    dbg_stop("att7")
                    P.op("dve", lambda e, a=a, bA=bA, bB=bB: e.tensor_tensor(out=a["t"][0:KB, :], in0=a["t"][0:KB, :], in1=bA[0:KB, NT:2 * NT], op=ALU.add), [bA, a["t"]], [a["t"]])
                    dbg_stop("att8")
                    P.op("act", lambda e, a=a, bA=bA, bB=bB: e.activation(out=a["w"][0:KB, :], in_=a["t"][0:KB, :], func=AF.Exp), [a["t"]], [a["w"]])
                    dbg_stop("att9")
                    if maskap is not None:
                        P.op("pool", lambda e, a=a, bA=bA, bB=bB: e.tensor_tensor(out=a["w"][0:KB, :], in0=a["w"][0:KB, :], in1=maskap, op=ALU.mult), [a["w"], maskP], [a["w"]])
                    dbg_stop("att10")
                    P.op("dve", lambda e, a=a, bA=bA, bB=bB: e.tensor_tensor(out=a["Cb"][:, :], in0=a["Cb"][:, :], in1=bB[:, 0:NT], op=ALU.add), [a["Cb"], bB], [a["Cb"]])
                    dbg_stop("att11")
                    P.op("pe", lambda e, a=a, hs=hs, bA=bA, bB=bB: e.matmul(bB[0:64, NT:2 * NT], lhsT=Vap_fn(hs), rhs=a["w"][0:KB, :], start=True, stop=True), [Vt, a["w"]], [bB])
                    if first:
                        P.op("dve", lambda e, hh=hh, bA=bA, bB=bB: e.tensor_copy(out=oacc[0:64, hh, :], in_=bB[0:64, NT:2 * NT]), [bB], [oacc])
                    else:
                        P.op("dve", lambda e, hh=hh, bA=bA, bB=bB: e.tensor_tensor(out=oacc[0:64, hh, :], in0=oacc[0:64, hh, :], in1=bB[0:64, NT:2 * NT], op=ALU.add), [bB, oacc], [oacc])
                done[0] += 1
            for b_ in range(nbc - 1, -1, -1):
                block(qkb, (lambda hs, b_=b_: qkb[hs, 1, b_ * KBc:(b_ + 1) * KBc]), vcur, (lambda hs, b_=b_: vcur[0:KBc, b_ * 128 + hs.start:b_ * 128 + hs.stop]), KBc, maskP[0:KBc, b_, 0:NT])
            for p0 in range(((npast + 3) // 4) * 4 - 4, -1, -4):
                nb_ = min(4, npast - p0)
                kp, vp = KP[kpi[0] % 2], VP[kpi[0] % 2]
                kpi[0] += 1
                if is_prompt:
                    P.dma("sp", kp[:, 0:nb_ * 128], khist_d.ap()[:, p0 * 128:(p0 + nb_) * 128], reads=[khist], writes=[kp])
                    P.dma("sp", vp[:, 0:nb_, :], vhist_d.ap()[:, p0:p0 + nb_, :], reads=[vhist], writes=[vp])
                else:
                    ks_, vs_ = WST[0], WST[0]
                    P.dma("sp", ks_[:, 0:nb_ * 128], din["kcache"].ap()[seq - 1][:, p0 * 128:(p0 + nb_) * 128], writes=[ks_])
                    P.op("pool", lambda e, kp=kp, ks_=ks_, nb_=nb_: e.tensor_copy(out=kp[:, 0:nb_ * 128], in_=ks_[:, 0:nb_ * 128]), [ks_], [kp])
                    P.dma("sp", vs_[:, 1024:1024 + nb_ * 128].rearrange("p (b d) -> p b d", b=nb_), din["vcache"].ap()[seq - 1][:, p0:p0 + nb_, :], writes=[vs_])
                    P.op("pool", lambda e, vp=vp, vs_=vs_, nb_=nb_: e.tensor_copy(out=vp[:, 0:nb_, :], in_=vs_[:, 1024:1024 + nb_ * 128].rearrange("p (b d) -> p b d", b=nb_)), [vs_], [vp])
                for b_ in range(nb_ - 1, -1, -1):
                    block(kp, (lambda hs, b_=b_, kp=kp: kp[hs, b_ * 128:(b_ + 1) * 128]), vp, (lambda hs, b_=b_, vp=vp: vp[:, b_, hs]), 128, None)
            dbg_stop("sb_att")
            osb = c.b[2]
            for hh in range(2):
                P.op("act", lambda e, hh=hh: e.activation(out=osb[0:64, hh, :], in_=oacc[0:64, hh, :], func=AF.Copy), [oacc], [osb])
            exchange_parts([(osb, osb[0:64, 0, :], 0, 64), (osb, osb[0:64, 1, :], 64, 128)], 128, NT, BF16, c.oall, c.oall[:, 0:8, :])
            yield
            out_proj_update(c, l, 0, seq, din["sw_out"].ap(), 8, 0)
            xchg_x(c)
            yield

        halfpi = P.sb([128, 1], F32)
        P.op("dve", lambda e: e.memset(halfpi[:, :], float(np.pi / 2)), [], [halfpi])
        C1 = 6.28125
        C2 = float(2 * np.pi - 6.28125)

        def retmix(c, l, seq, t0, si):
            NT, C, n = c.NT, c.C, c.n
            norm_mod(c, l, 0, seq)
            qk32 = c.f[1]
            vb = c.b[1]
            gs = c.f[3]

            def evac(fcx, pb):
                if fcx < 4:
                    P.op("act", lambda e: e.activation(out=qk32[:, fcx, :], in_=pb[:, 0:NT], func=AF.Copy), [pb], [qk32])
                elif fcx < 8:
                    P.op("act", lambda e: e.activation(out=vb[:, fcx - 4, :], in_=pb[:, 0:NT], func=AF.Copy), [pb], [vb])
                else:
                    P.op("act", lambda e: e.activation(out=gs[:, fcx - 8, :], in_=pb[:, 0:NT], func=AF.Silu), [pb], [gs])
            proj(c, din["rw_in"].ap(), 1536, evac)
            pos0 = float(t0 if seq == 0 else PAST)
            tr = c.f[2]
            sc = c.f[4]
            ni = c.ni
            P.op("dve", lambda e: e.tensor_scalar(out=tr[:, 0, :], in0=iota[:, 0:NT], scalar1=pos0, scalar2=invf[:, 0:1], op0=ALU.add, op1=ALU.mult), [iota, invf], [tr])
            P.op("dve", lambda e: e.tensor_scalar(out=tr[:, 2, :], in0=tr[:, 0, :], scalar1=halfpi[:, 0:1], scalar2=None, op0=ALU.add), [tr, halfpi], [tr])
            for src_i, dst_i in ((0, 0), (2, 1)):
                P.op("dve", lambda e, src_i=src_i: e.tensor_scalar(out=ni[:, :], in0=tr[:, src_i, :], scalar1=float(1.0 / (2 * np.pi)), scalar2=None, op0=ALU.mult), [tr], [ni])
                P.op("dve", lambda e: e.tensor_copy(out=tr[:, 1, :], in_=ni[:, :]), [ni], [tr])
                P.op("dve", lambda e, src_i=src_i: e.scalar_tensor_tensor(out=tr[:, 3, :], in0=tr[:, 1, :], scalar=-C1, in1=tr[:, src_i, :], op0=ALU.mult, op1=ALU.add), [tr], [tr])
                P.op("dve", lambda e: e.scalar_tensor_tensor(out=tr[:, 3, :], in0=tr[:, 1, :], scalar=-C2, in1=tr[:, 3, :], op0=ALU.mult, op1=ALU.add), [tr], [tr])
                P.op("act", lambda e, dst_i=dst_i: e.activation(out=sc[:, dst_i, :], in_=tr[:, 3, :], func=AF.Sin), [tr], [sc])
            qr = c.b[2]
            qd = c.b[3]
            for base, scale in ((0, 1.0), (2, 256.0 ** -0.5)):
                x1, x2 = qk32[:, base, :], qk32[:, base + 1, :]
                P.op("dve", lambda e, x1=x1: e.tensor_tensor(out=sc[:, 2, :], in0=x1, in1=sc[:, 1, :], op=ALU.mult), [qk32, sc], [sc])
                P.op("pool", lambda e, x2=x2: e.tensor_tensor(out=sc[:, 3, :], in0=x2, in1=sc[:, 0, :], op=ALU.mult), [qk32, sc], [sc])
                P.op("dve", lambda e, base=base, scale=scale: e.scalar_tensor_tensor(out=qr[:, base, :], in0=sc[:, 2, :], scalar=1.0, in1=sc[:, 3, :], op0=ALU.mult, op1=ALU.subtract), [sc], [qr])
                P.op("dve", lambda e, x1=x1: e.tensor_tensor(out=sc[:, 2, :], in0=x1, in1=sc[:, 0, :], op=ALU.mult), [qk32, sc, qr], [sc])
                P.op("pool", lambda e, x2=x2: e.tensor_tensor(out=sc[:, 3, :], in0=x2, in1=sc[:, 1, :], op=ALU.mult), [qk32, sc, qr], [sc])
                P.op("dve", lambda e, base=base: e.tensor_tensor(out=qr[:, base + 1, :], in0=sc[:, 2, :], in1=sc[:, 3, :], op=ALU.add), [sc], [qr])
                if scale != 1.0:
                    P.op("pool", lambda e, base=base, scale=scale: e.tensor_scalar(out=qr[:, base:base + 2, :], in0=qr[:, base:base + 2, :], scalar1=scale, scalar2=None, op0=ALU.mult), [qr], [qr])
            dbg_stop("ret_rot")
            qdec_ap = rconst[:, 0:NT] if C == 64 else rconst[:, 512:512 + NT]
            for dk in range(2):
                P.op("dve", lambda e, dk=dk: e.tensor_tensor(out=qd[:, dk, :], in0=qr[:, dk, :], in1=qdec_ap, op=ALU.mult), [qr, rconst], [qd])
            kdt, vtk = c.rk, c.rv
            kdec_ap = rconst[0:C, 640:641] if C == 64 else rconst[0:C, 641:642]
            for m in range(n):
                cs = slice(m * C, (m + 1) * C)
                pt = bbank()
                for dk in range(2):
                    P.op("pe", lambda e, dk=dk, cs=cs, pt=pt: e.transpose(pt[0:C, dk * 128:(dk + 1) * 128], qr[:, 2 + dk, cs], ident_b[:, :]), [qr, ident_b], [pt])
                for dv in range(4):
                    P.op("pe", lambda e, dv=dv, cs=cs, pt=pt: e.transpose(pt[0:C, 256 + dv * 128:256 + (dv + 1) * 128], vb[:, dv, cs], ident_b[:, :]), [vb, ident_b], [pt])
                P.op("dve", lambda e, m=m, pt=pt: e.tensor_copy(out=kdt[0:C, m * 256:(m + 1) * 256], in_=pt[0:C, 0:256]), [pt], [kdt])
                P.op("dve", lambda e, m=m, pt=pt: e.tensor_copy(out=vtk[0:C, m * 512:(m + 1) * 512], in_=pt[0:C, 256:768]), [pt], [vtk])
                P.op("pool", lambda e, m=m: e.tensor_scalar(out=kdt[0:C, m * 256:(m + 1) * 256], in0=kdt[0:C, m * 256:(m + 1) * 256], scalar1=kdec_ap, scalar2=None, op0=ALU.mult), [kdt, rconst], [kdt])
            dbg_stop("ret_tok")
            pq = PS[2]
            for m in range(n):
                cs = slice(m * C, (m + 1) * C)
                for dk in range(2):
                    P.op("pe", lambda e, dk=dk, cs=cs, pq=pq: e.matmul(pq[0:C, cs], lhsT=qr[:, 2 + dk, cs], rhs=qr[:, dk, cs], start=(dk == 0), stop=(dk == 1)), [qr], [pq])
            AT = c.smb[5]
            DT_ap = rconst[0:C, 544:544 + C] if C == 64 else rconst[0:C, 608:608 + C]
            for m in range(n):
                cs = slice(m * C, (m + 1) * C)
                P.op("dve", lambda e, cs=cs, pq=pq: e.tensor_tensor(out=AT[0:C, cs], in0=pq[0:C, cs], in1=DT_ap, op=ALU.mult), [pq, rconst], [AT])
            R, Rb = c.R, c.Rb
            gC_ap = rconst[:, 642:643] if C == 64 else rconst[:, 643:644]

            def oacc(dv):
                return PLONG[dv // 2], (dv % 2) * NT
            for m in range(n):
                cs = slice(m * C, (m + 1) * C)
                for dv in range(4):
                    pl, off = oacc(dv)
                    P.op("pe", lambda e, dv=dv, cs=cs, m=m, pl=pl, off=off: e.matmul(pl[:, off + m * C:off + (m + 1) * C], lhsT=vtk[0:C, m * 512 + dv * 128:m * 512 + (dv + 1) * 128], rhs=AT[0:C, cs], start=True, stop=False), [vtk, AT], [pl])
                    for dk in range(2):
                        P.op("pe", lambda e, dv=dv, dk=dk, cs=cs, m=m, pl=pl, off=off: e.matmul(pl[:, off + m * C:off + (m + 1) * C], lhsT=Rb[:, dk, dv * 128:(dv + 1) * 128], rhs=qd[:, dk, cs], start=False, stop=(dk == 1)), [Rb, qd], [pl])
                for dk in range(2):
                    pR = PS[dk]
                    P.op("pe", lambda e, dk=dk, m=m, pR=pR: e.matmul(pR[:, 0:512], lhsT=kdt[0:C, m * 256 + dk * 128:m * 256 + (dk + 1) * 128], rhs=vtk[0:C, m * 512:(m + 1) * 512], start=True, stop=True), [kdt, vtk], [pR])
                    P.op("dve", lambda e, dk=dk, pR=pR: e.scalar_tensor_tensor(out=R[:, dk, :], in0=R[:, dk, :], scalar=gC_ap, in1=pR[:, 0:512], op0=ALU.mult, op1=ALU.add), [R, rconst, pR], [R])
                    P.op("act", lambda e, dk=dk: e.activation(out=Rb[:, dk, :], in_=R[:, dk, :], func=AF.Copy), [R], [Rb])
            dbg_stop("ret_scan")
            o32 = c.f[0]
            for dv in range(4):
                pl, off = oacc(dv)
                P.op("act", lambda e, dv=dv, pl=pl, off=off: e.activation(out=o32[:, dv, :], in_=pl[:, off:off + NT], func=AF.Copy), [pl], [o32])
            pm = PS[3]
            for dv in range(4):
                P.op("pe", lambda e, dv=dv, pm=pm: e.matmul(pm[:, 0:NT], lhsT=ones_f[:, :], rhs=o32[:, dv, :], start=(dv == 0), stop=(dv == 3)), [ones_f, o32], [pm])
            dd = c.f[2]
            dsq = c.f[4]
            for dv in range(4):
                P.op("dve", lambda e, dv=dv, pm=pm: e.scalar_tensor_tensor(out=dd[:, dv, :], in0=pm[:, 0:NT], scalar=-1.0 / 512, in1=o32[:, dv, :], op0=ALU.mult, op1=ALU.add), [pm, o32], [dd])
            P.op("act", lambda e: e.activation(out=dsq[:, :, :], in_=dd[:, :, :], func=AF.Square), [dd], [dsq])
            pv = PS[2]
            for dv in range(4):
                P.op("pe", lambda e, dv=dv, pv=pv: e.matmul(pv[:, 0:NT], lhsT=ones_f[:, :], rhs=dsq[:, dv, :], start=(dv == 0), stop=(dv == 3)), [ones_f, dsq], [pv])
            rs_ = c.rstd
            rsqrt(rs_, rs_[:, :], pv, pv[:, 0:NT], 4)
            on = c.b[0]
            for dv in range(2):
                P.op("dve", lambda e, dv=dv: e.scalar_tensor_tensor(out=dd[:, dv, :], in0=dd[:, dv, :], scalar=512.0 ** 0.5, in1=rs_[:, :], op0=ALU.mult, op1=ALU.mult), [dd, rs_], [dd])
                P.op("dve", lambda e, dv=dv: e.scalar_tensor_tensor(out=on[:, dv, :], in0=dd[:, dv, :], scalar=rng_t[:, dv:dv + 1], in1=gs[:, dv, :], op0=ALU.mult, op1=ALU.mult), [dd, rng_t, gs], [on])
            exchange(on, on[:, 0:2, :], 2, NT, BF16, c.oall, c.oall[:, 0:16, :])
            yield
            out_proj_update(c, l, 0, seq, din["rw_out"].ap(), 16, 0)
            xchg_x(c)
            yield

        def load_x(c, src_full_ap, src_own_ap):
            P.dma("sp", c.xfull[:, :, :], src_full_ap, writes=[c.xfull])
            P.dma("sp", c.xown[:, :], src_own_ap, writes=[c.xown])

        def prompt_thread():
            c = make_ctx(NTP, 64, "p")
            for j in range(2):
                P.op("dve", lambda e, j=j: e.memset(c.S[j][:, :], 0.0), [], [c.S[j]])
                P.op("dve", lambda e, j=j: e.memset(c.Sb[j][:, :], 0.0), [], [c.Sb[j]])
            P.op("dve", lambda e: e.memset(c.R[:, :, :], 0.0), [], [c.R])
            P.op("dve", lambda e: e.memset(c.Rb[:, :, :], 0.0), [], [c.Rb])
            for j in range(2):
                P.op("dve", lambda e, j=j: e.memset(c.gtail[j][:, :, :], 0.0), [], [c.gtail[j]])
            for l in range(4):
                P.op("dve", lambda e, l=l: e.memset(c.ftail[l][:, :, :], 0.0), [], [c.ftail[l]])
            for i in range(NSTEP):
                t0 = i * NTP
                load_x(c, din["xp_full"].ap()[:, :, t0:t0 + NTP], din["xp_own"].ap()[:, t0:t0 + NTP])
                yield from step(c, 0, t0, t0, i == 0, i == NSTEP - 1, i)
            for j in range(2):
                P.dma("sp", dout["oS"].ap()[j, 0], c.S[j][:, :], reads=[c.S[j]], writes=[tout["oS"]])
                P.dma("sp", dout["oconv"].ap()[j, 0], c.gtail[j][:, :, :], reads=[c.gtail[j]], writes=[tout["oconv"]])
            for l in range(NL):
                P.dma("sp", dout["ofc"].ap()[l, 0], c.ftail[l][:, :, :], reads=[c.ftail[l]], writes=[tout["ofc"]])
            P.dma("sp", dout["oR"].ap()[0], c.R[:, :, :], reads=[c.R], writes=[tout["oR"]])

        def sample_thread(bs):
            c = make_ctx(SL, 32, "s")
            for b in bs:
                seq = b + 1
                for j in range(2):
                    P.dma("sp", c.S[j][:, :], din["sgS"].ap()[j, b], writes=[c.S[j]])
                    P.op("pool", lambda e, j=j: e.tensor_copy(out=c.Sb[j][:, :], in_=c.S[j][:, :]), [c.S[j]], [c.Sb[j]])
                    P.dma("sp", c.gtail[j][:, :, :], din["sgconv"].ap()[j][:, b], writes=[c.gtail[j]])
                for l in range(4):
                    P.dma("sp", c.ftail[l][:, :, :], din["sfconv"].ap()[l][:, b], writes=[c.ftail[l]])
                P.dma("sp", c.R[:, :, :], din["sR"].ap()[b], writes=[c.R])
                P.op("pool", lambda e: e.tensor_copy(out=c.Rb[:, :, :], in_=c.R[:, :, :]), [c.R], [c.Rb])
                load_x(c, din["xs_full"].ap()[:, :, b * SL:(b + 1) * SL], din["xs_own"].ap()[:, b * SL:(b + 1) * SL])
                yield from step(c, seq, PAST, SEQ + b * SL, True, True, 0)
                for j in range(2):
                    P.dma("sp", dout["oS"].ap()[j, seq], c.S[j][:, :], reads=[c.S[j]], writes=[tout["oS"]])
                    P.dma("sp", dout["oconv"].ap()[j, seq], c.gtail[j][:, :, :], reads=[c.gtail[j]], writes=[tout["oconv"]])
                for l in range(NL):
                    P.dma("sp", dout["ofc"].ap()[l, seq], c.ftail[l][:, :, :], reads=[c.ftail[l]], writes=[tout["ofc"]])
                P.dma("sp", dout["oR"].ap()[seq], c.R[:, :, :], reads=[c.R], writes=[tout["oR"]])

        threads = []
        if NSTEP > 0:
            threads.append(prompt_thread())
        if DO_SAMPLE:
            threads.append(sample_thread(list(range(NS_RUN))))
        live = list(threads)
        while live:
            for g in list(live):
                try:
                    next(g)
                except StopIteration:
                    live.remove(g)
        if DBG.get("mem"):
            print("SBUF bytes/partition:", P.sbytes)
            for x in sorted(P.big, reverse=True)[:40]:
                print("   ", x)
        P.wait_all("sp", list(tout.values()))
        P.wait_all("pool", list(tout.values()))
        P.drain()
        P.emit()
    return nc


def assemble(results):
    R_ = results
    f = np.float32
    y = np.concatenate([R_[r]["y_own"] for r in range(8)], axis=0)
    y_prompt = np.ascontiguousarray(y[:, :SEQ].T)[None]
    y_sample = np.ascontiguousarray(y[:, SEQ:].T).reshape(NSMP, SL, D)
    oS = np.stack([R_[r]["oS"] for r in range(8)])
    oconv = np.stack([R_[r]["oconv"] for r in range(8)])
    ok = np.stack([R_[r]["ok"] for r in range(8)])
    ov = np.stack([R_[r]["ov"] for r in range(8)])
    oR = np.stack([R_[2 * hh]["oR"] for hh in range(4)])
    ofc = np.stack([R_[r]["ofc"] for r in range(8)])

    def S_of(j, sl):
        return np.ascontiguousarray(oS[:, j, sl].transpose(1, 0, 2, 3))

    def conv_of(j, sl):
        a = oconv[:, j, sl]
        return np.ascontiguousarray(a.transpose(1, 4, 3, 0, 2).reshape(a.shape[1], 3, 3072))

    def kv_of(a, c0, c1, nb):
        x = a[:, :, c0:c1].reshape(8, 2, 64, nb, (c1 - c0) // nb)
        return np.ascontiguousarray(x.transpose(3, 4, 0, 1, 2).reshape(nb, (c1 - c0) // nb, 16, 64))

    def R_of(sl):
        a = oR[:, sl]
        return np.ascontiguousarray(a.transpose(1, 0, 3, 2, 4).reshape(a.shape[1], 4, 256, 512))

    def fc_of(sl):
        a = ofc[:, :, sl]
        nseq = a.shape[2]
        out = np.zeros((4, nseq, 2, 2 * DFF), f)
        pad = a.transpose(1, 2, 5, 0, 4, 3).reshape(4, nseq, 2, 8, 768)
        for r in range(8):
            out[:, :, :, FOWN * r:FOWN * r + FOWN] = pad[:, :, :, r, 0:FOWN]
            out[:, :, :, DFF + FOWN * r:DFF + FOWN * r + FOWN] = pad[:, :, :, r, 384:384 + FOWN]
        return out
    p = slice(0, 1)
    sm = slice(1, 17)
    return (y_prompt.astype(f), y_sample.astype(f),
            S_of(0, p), conv_of(0, p), kv_of(ok, 0, SEQ, 1), kv_of(ov, 0, SEQ, 1), R_of(p), S_of(1, p), conv_of(1, p), fc_of(p),
            S_of(0, sm), conv_of(0, sm), kv_of(ok, SEQ, SEQ + 512, NSMP), kv_of(ov, SEQ, SEQ + 512, NSMP), R_of(sm), S_of(1, sm), conv_of(1, sm), fc_of(sm))


def kernel(**inputs):
    inp = {k: np.asarray(v) for k, v in inputs.items()}
    res = run_all(inp)
    return assemble(res.results)


def run_all(inp, NSTEP=SEQ // NTP, NL=4, DO_SAMPLE=True, trace=False, lite=False, NS_RUN=NSMP):
    consts = host_consts()
    xp_full = np.ascontiguousarray(inp["x_prompt"][0].T.reshape(8, 128, SEQ).transpose(1, 0, 2))
    xs_full = np.ascontiguousarray(inp["x_sample"].reshape(512, D).T.reshape(8, 128, 512).transpose(1, 0, 2))
    in_maps = []
    for r in range(NCORE):
        m = dict(consts)
        m.update(host_core_inputs(inp, r))
        m["xp_full"] = xp_full
        m["xs_full"] = xs_full
        in_maps.append(m)
    if lite:
        ntok = NSTEP * NTP
        for m in in_maps:
            m["xp_full"] = np.ascontiguousarray(m["xp_full"][:, :, :ntok])
            m["xp_own"] = np.ascontiguousarray(m["xp_own"][:, :ntok])
            m["adaw"] = np.ascontiguousarray(m["adaw"][:max(NL, 1)])
            if not DO_SAMPLE:
                for k in ("kcache", "vcache", "sR", "sgS"):
                    m[k] = np.ascontiguousarray(m[k][..., :1, :, :, :] if False else m[k][:1] if k != "sgS" else m[k][:, :1])
    shapes = {k: v.shape for k, v in in_maps[0].items()}
    nc = build(shapes, NSTEP=NSTEP, NL=NL, DO_SAMPLE=DO_SAMPLE, NS_RUN=NS_RUN)
    res = run_bass_kernel_spmd(nc, in_maps, core_ids=list(range(NCORE)), **({"trace": True} if trace else {}))
    return res
```
